# Optimizing a Trainium2 kernel written in Bass

```python
import jax, jax.numpy as jnp
from jax import lax
import numpy as np

D_MODEL = 2048
BATCH = 2
SEQ = 4096
DEPTH = 4

CHUNK = 64
EPS = 1e-6
D_FF = ((8 * D_MODEL // 3 + 127) // 128) * 128
D_CONV = D_MODEL // 2
CONV_WIDTH = 31
POOL_WINDOWS = (2, 4, 8, 16)
N_POOL_GROUPS = len(POOL_WINDOWS)
D_POOL = D_MODEL // 2
POOL_GROUP = D_POOL // N_POOL_GROUPS
D_AB_IN = 2 * D_CONV + D_POOL
D_AB_OUT = D_CONV + D_POOL
SGU_LEN = 128
D_SGU = D_MODEL
SGU_HEADS = 8
SGU_HEAD_DIM = D_SGU // SGU_HEADS
N_EVEN = (DEPTH + 1) // 2
N_ODD = DEPTH // 2

kernel_name = 'hybrid_conv_pool_sgu_macaron_trunk'


def _rms_norm(x, g):
    xf = x.astype(jnp.float32)
    y = xf * lax.rsqrt(jnp.mean(xf * xf, axis=-1, keepdims=True) + EPS)
    return (y * g.astype(jnp.float32)).astype(x.dtype)


def _layer_norm(x, g, b):
    xf = x.astype(jnp.float32)
    mu = jnp.mean(xf, axis=-1, keepdims=True)
    xc = xf - mu
    var = jnp.mean(xc * xc, axis=-1, keepdims=True)
    y = xc * lax.rsqrt(var + EPS) * g.astype(jnp.float32) + b.astype(jnp.float32)
    return y.astype(x.dtype)


def _swiglu(h, w_gate, w_up, w_down):
    return (jax.nn.silu(h @ w_gate) * (h @ w_up)) @ w_down


def _causal_depthwise_conv(a, w, b):
    k = w.shape[0]
    y = lax.conv_general_dilated(
        a, w[:, None, :].astype(a.dtype), window_strides=(1,), padding=[(k - 1, 0)],
        dimension_numbers=('NWC', 'WIO', 'NWC'), feature_group_count=a.shape[-1])
    return y + b


def _multiscale_pool(p):
    bsz, s, _ = p.shape
    pg = p.reshape(bsz, s, N_POOL_GROUPS, POOL_GROUP).astype(jnp.float32)
    cs = jnp.cumsum(pg, axis=1)
    counts = jnp.arange(1, s + 1, dtype=jnp.float32)
    outs = []
    for g, w in enumerate(POOL_WINDOWS):
        c = cs[:, :, g]
        lagged = jnp.pad(c, ((0, 0), (w, 0), (0, 0)))[:, :s]
        cnt = jnp.minimum(counts, jnp.float32(w))[None, :, None]
        outs.append((c - lagged) / cnt)
    mean = jnp.stack(outs, axis=2)
    return (mean - pg).astype(p.dtype)


def _conv_pool_mixer(h, w_in, conv_w, conv_b, ln_g, ln_b, pool_w, pool_scale, w_out):
    z = h @ w_in
    a, gate, p = jnp.split(z, [D_CONV, 2 * D_CONV], axis=-1)
    a = a * jax.nn.sigmoid(gate)
    a = _causal_depthwise_conv(a, conv_w, conv_b)
    a = jax.nn.silu(_layer_norm(a, ln_g, ln_b))
    pooled = _multiscale_pool(p)
    pb = jnp.einsum('bsgc,gcd->bsgd', pooled, pool_w).reshape(p.shape) * pool_scale
    return jnp.concatenate([a, pb], axis=-1) @ w_out


def _sgu_mixer(h, w_in, ln_g, ln_b, sgu_w, sgu_b, w_out):
    bsz, s, _ = h.shape
    z = jax.nn.gelu(h @ w_in)
    u, v = jnp.split(z, 2, axis=-1)
    v = _layer_norm(v, ln_g, ln_b)
    v = v.reshape(bsz, s // SGU_LEN, SGU_LEN, SGU_HEADS, SGU_HEAD_DIM)
    pos = jnp.arange(SGU_LEN)
    mask = (pos[None, :] // CHUNK) <= (pos[:, None] // CHUNK)
    w = jnp.where(mask[None], sgu_w, 0)
    mixed = jnp.einsum('gij,bnjgc->bnigc', w, v) + sgu_b.T[None, None, :, :, None]
    return (u * mixed.reshape(bsz, s, D_SGU)) @ w_out


def setup_inputs(seed: int = 0) -> dict:
    key = jax.random.key(seed)
    ks = iter(jax.random.split(key, 32))

    def nrm(shape, scale):
        return jax.random.normal(next(ks), shape, jnp.float32) * scale

    def gain(shape):
        return 1.0 + 0.1 * jax.random.normal(next(ks), shape, jnp.float32)

    return {
        'x': nrm((BATCH, SEQ, D_MODEL), 1.0),
        'norm_ffn1': gain((DEPTH, D_MODEL)),
        'ffn1_w_gate': nrm((DEPTH, D_MODEL, D_FF), D_MODEL ** -0.5),
        'ffn1_w_up': nrm((DEPTH, D_MODEL, D_FF), D_MODEL ** -0.5),
        'ffn1_w_down': nrm((DEPTH, D_FF, D_MODEL), D_FF ** -0.5),
        'norm_mix': gain((DEPTH, D_MODEL)),
        'norm_ffn2': gain((DEPTH, D_MODEL)),
        'ffn2_w_gate': nrm((DEPTH, D_MODEL, D_FF), D_MODEL ** -0.5),
        'ffn2_w_up': nrm((DEPTH, D_MODEL, D_FF), D_MODEL ** -0.5),
        'ffn2_w_down': nrm((DEPTH, D_FF, D_MODEL), D_FF ** -0.5),
        'ab_w_in': nrm((N_EVEN, D_MODEL, D_AB_IN), D_MODEL ** -0.5),
        'conv_w': nrm((N_EVEN, CONV_WIDTH, D_CONV), CONV_WIDTH ** -0.5),
        'conv_b': nrm((N_EVEN, D_CONV), 0.02),
        'conv_ln_g': gain((N_EVEN, D_CONV)),
        'conv_ln_b': nrm((N_EVEN, D_CONV), 0.02),
        'pool_w': nrm((N_EVEN, N_POOL_GROUPS, POOL_GROUP, POOL_GROUP), POOL_GROUP ** -0.5),
        'pool_scale': gain((N_EVEN, D_POOL)),
        'ab_w_out': nrm((N_EVEN, D_AB_OUT, D_MODEL), D_AB_OUT ** -0.5),
        'c_w_in': nrm((N_ODD, D_MODEL, 2 * D_SGU), D_MODEL ** -0.5),
        'sgu_ln_g': gain((N_ODD, D_SGU)),
        'sgu_ln_b': nrm((N_ODD, D_SGU), 0.02),
        'sgu_w': nrm((N_ODD, SGU_HEADS, SGU_LEN, SGU_LEN), 0.5 * SGU_LEN ** -0.5),
        'sgu_b': gain((N_ODD, SGU_HEADS, SGU_LEN)),
        'c_w_out': nrm((N_ODD, D_SGU, D_MODEL), D_SGU ** -0.5),
        'final_norm': gain((D_MODEL,)),
    }


def reference(x, norm_ffn1, ffn1_w_gate, ffn1_w_up, ffn1_w_down, norm_mix,
              norm_ffn2, ffn2_w_gate, ffn2_w_up, ffn2_w_down,
              ab_w_in, conv_w, conv_b, conv_ln_g, conv_ln_b, pool_w, pool_scale, ab_w_out,
              c_w_in, sgu_ln_g, sgu_ln_b, sgu_w, sgu_b, c_w_out, final_norm):
    for layer in range(DEPTH):
        x = x + 0.5 * _swiglu(_rms_norm(x, norm_ffn1[layer]),
                              ffn1_w_gate[layer], ffn1_w_up[layer], ffn1_w_down[layer])
        h = _rms_norm(x, norm_mix[layer])
        i = layer // 2
        if layer % 2 == 0:
            x = x + _conv_pool_mixer(h, ab_w_in[i], conv_w[i], conv_b[i], conv_ln_g[i],
                                     conv_ln_b[i], pool_w[i], pool_scale[i], ab_w_out[i])
        else:
            x = x + _sgu_mixer(h, c_w_in[i], sgu_ln_g[i], sgu_ln_b[i], sgu_w[i],
                               sgu_b[i], c_w_out[i])
        x = x + 0.5 * _swiglu(_rms_norm(x, norm_ffn2[layer]),
                              ffn2_w_gate[layer], ffn2_w_up[layer], ffn2_w_down[layer])
    return _rms_norm(x, final_norm)
```

```python
import numpy as np
import concourse.bass as bass
import concourse.mybir as mybir
from concourse.bass_utils import run_bass_kernel_spmd

F32 = mybir.dt.float32
BF16 = mybir.dt.bfloat16
AF = mybir.ActivationFunctionType
ALU = mybir.AluOpType

EPS = 1e-6
CONV_W = 31
PIECE = 2048
HALO = 160


class Cfg:
    def __init__(s, D=2048, DFF=5504, SEQ=4096, BATCH=2, DEPTH=4, NCORES=8,
                 G=4, NR=10, NS=3):
        s.D, s.DFF, s.SEQ, s.BATCH, s.DEPTH, s.NCORES = D, DFF, SEQ, BATCH, DEPTH, NCORES
        s.KD = D // 128
        s.NF = DFF // 128
        s.CPS = NCORES // BATCH
        s.OWN = SEQ // s.CPS
        s.HALF = s.OWN // 2
        s.TA = HALO + s.HALF
        s.TB = s.HALF
        s.KC = (D // 2) // 128
        s.KP = (D // 2) // 128
        s.GP = s.KP // 4
        s.HC = s.KD // 8
        s.G = G
        s.NR, s.NS = NR, NS
        s.N_EVEN = (DEPTH + 1) // 2
        s.N_ODD = DEPTH // 2
        assert s.KD * 128 <= PIECE and D <= PIECE and s.HALF % 128 == 0
        assert DEPTH == 4


def ffn_groups(cfg):
    sizes = [cfg.G] * (cfg.NF // cfg.G)
    rem = cfg.NF % cfg.G
    if rem:
        sizes.append(rem)
    if len(sizes) >= 2 and sizes[-1] == 1 and cfg.G >= 3:
        sizes[-2:] = [cfg.G - 1, 2]
    out, i = [], 0
    for n in sizes:
        out.append(list(range(i, i + n)))
        i += n
    assert i == cfg.NF
    return out


def ffn_order(cfg, l, w):
    groups = ffn_groups(cfg)
    seq = []

    def GU(g):
        for f in groups[g]:
            seq.append(("ffn", l, w, "g", f))
            seq.append(("ffn", l, w, "u", f))

    def DN(g):
        for f in groups[g]:
            seq.append(("ffn", l, w, "d", f))

    GU(0)
    for g in range(1, len(groups)):
        GU(g)
        DN(g - 1)
    DN(len(groups) - 1)
    return seq


def about_korder(cfg):
    return list(range(cfg.KC, 2 * cfg.KC)) + list(range(cfg.KC))


def mixer_order(cfg, l):
    i = l // 2
    seq = []
    if l % 2 == 0:
        for j in range(cfg.KC):
            seq.append(("abin", i, j))
            seq.append(("abin", i, cfg.KC + j))
        for grp in range(4):
            for jj in range(cfg.GP):
                seq.append(("abin", i, 2 * cfg.KC + grp * cfg.GP + jj))
            if grp == 0:
                seq.append(("poolw", i))
        for kc in about_korder(cfg):
            seq.append(("about", i, kc))
    else:
        for c in range(cfg.KD):
            seq.append(("cin", i, "v1", c))
        for c in range(cfg.KD):
            seq.append(("cin", i, "v", c))
        for c in range(cfg.KD):
            seq.append(("cin", i, "u", c))
        for kc in range(cfg.KD):
            seq.append(("cout", i, kc))
    return seq


def piece_order(cfg):
    seq = []
    for l in range(cfg.DEPTH):
        seq += ffn_order(cfg, l, 1)
        seq += mixer_order(cfg, l)
        seq += ffn_order(cfg, l, 2)
    return seq


def _colchunk(W, c, KD):
    blk = W[:, c * 128:(c + 1) * 128]
    return blk.reshape(KD, 128, 128).transpose(1, 0, 2).reshape(128, KD * 128)


def pack_weights(cfg, inp):
    order = piece_order(cfg)
    out = np.zeros((len(order), 128, PIECE), np.float32)
    KD = cfg.KD
    for n, tag in enumerate(order):
        if tag[0] == "ffn":
            _, l, w, kind, f = tag
            if kind == "g":
                W = inp["ffn1_w_gate"] if w == 1 else inp["ffn2_w_gate"]
                a = _colchunk(W[l], f, KD)
            elif kind == "u":
                W = inp["ffn1_w_up"] if w == 1 else inp["ffn2_w_up"]
                a = _colchunk(W[l], f, KD)
            else:
                W = inp["ffn1_w_down"] if w == 1 else inp["ffn2_w_down"]
                a = W[l][f * 128:(f + 1) * 128, :]
        elif tag[0] == "abin":
            a = _colchunk(inp["ab_w_in"][tag[1]], tag[2], KD)
        elif tag[0] == "poolw":
            pw = inp["pool_w"][tag[1]]
            PG = pw.shape[1]
            a = pw.reshape(4, cfg.GP, 128, PG).transpose(2, 0, 1, 3).reshape(128, -1)
        elif tag[0] == "about":
            a = inp["ab_w_out"][tag[1]][tag[2] * 128:(tag[2] + 1) * 128, :]
        elif tag[0] == "cin":
            _, i, kind, c = tag
            cc = c + (KD if kind in ("v", "v1") else 0)
            a = _colchunk(inp["c_w_in"][i], cc, KD)
        elif tag[0] == "cout":
            a = inp["c_w_out"][tag[1]][tag[2] * 128:(tag[2] + 1) * 128, :]
        else:
            raise ValueError(tag)
        out[n, :, :a.shape[1]] = a
    return out


class ParLayout:
    def __init__(s, cfg):
        s.off = {}
        n = 0

        def add(name, w):
            nonlocal n
            s.off[name] = (n, w)
            n += w

        for l in range(cfg.DEPTH):
            for w in range(3):
                add(("norm", l, w), cfg.KD)
        add(("fnorm",), cfg.KD)
        for i in range(cfg.N_EVEN):
            add(("convw", i), cfg.KC * CONV_W)
            add(("convb", i), cfg.KC)
            add(("clng", i), cfg.KC)
            add(("clnb", i), cfg.KC)
            add(("pscale", i), cfg.KP)
        for i in range(cfg.N_ODD):
            add(("slng", i), cfg.KD)
            add(("slnb", i), cfg.KD)
        add(("hmask",), HALO)
        add(("pcorr",), 4 * 16)
        add(("ident",), 128)
        s.n = n


def _fm(v):
    return np.ascontiguousarray(v.reshape(-1, 128).T)


def pack_params(cfg, inp, core):
    L = ParLayout(cfg)
    P = np.zeros((128, L.n), np.float32)

    def put(name, arr):
        o, w = L.off[name]
        assert arr.shape == (128, w), (name, arr.shape, w)
        P[:, o:o + w] = arr

    names = ["norm_ffn1", "norm_mix", "norm_ffn2"]
    for l in range(cfg.DEPTH):
        for w in range(3):
            put(("norm", l, w), _fm(inp[names[w]][l]))
    put(("fnorm",), _fm(inp["final_norm"]))
    for i in range(cfg.N_EVEN):
        cw = inp["conv_w"][i]
        a = cw.reshape(CONV_W, cfg.KC, 128).transpose(2, 1, 0).reshape(128, cfg.KC * CONV_W)
        put(("convw", i), a)
        put(("convb", i), _fm(inp["conv_b"][i]))
        put(("clng", i), _fm(inp["conv_ln_g"][i]))
        put(("clnb", i), _fm(inp["conv_ln_b"][i]))
        put(("pscale", i), _fm(inp["pool_scale"][i]))
    for i in range(cfg.N_ODD):
        put(("slng", i), _fm(inp["sgu_ln_g"][i]))
        put(("slnb", i), _fm(inp["sgu_ln_b"][i]))
    first = (core % cfg.CPS) == 0
    put(("hmask",), np.full((128, HALO), 0.0 if first else 1.0, np.float32))
    corr = np.ones((4, 16), np.float32)
    if first:
        for g, w in enumerate((2, 4, 8, 16)):
            for t in range(16):
                corr[g, t] = float(w) / float(min(t + 1, w))
    put(("pcorr",), np.broadcast_to(corr.reshape(1, 64), (128, 64)))
    put(("ident",), np.eye(128, dtype=np.float32))
    return P


def pack_oddp(cfg, inp):
    out = np.zeros((cfg.N_ODD, 128, 2048), np.float32)
    for i in range(cfg.N_ODD):
        w = inp["sgu_w"][i]
        out[i, :, :1024] = w.transpose(2, 0, 1).reshape(128, 1024)
        out[i, :, 1024:] = np.broadcast_to(inp["sgu_b"][i].reshape(1, 1024), (128, 1024))
    return out


def pack_x(cfg, x, core):
    b = core // cfg.CPS
    s0 = (core % cfg.CPS) * cfg.OWN
    T = HALO + cfg.OWN
    buf = np.zeros((T, cfg.D), np.float32)
    lo = s0 - HALO
    if lo >= 0:
        buf[:] = x[b, lo:lo + T]
    else:
        buf[-lo:] = x[b, 0:T + lo]
    return np.ascontiguousarray(buf.reshape(T, cfg.KD, 128).transpose(2, 1, 0))


class Tok:
    __slots__ = ("sem", "val", "eng")

    def __init__(s, sem, val, eng):
        s.sem, s.val, s.eng = sem, val, eng


class Eng:
    def __init__(s, name, sem, self_sync):
        s.name, s.sem, s.self_sync = name, sem, self_sync
        s.ops = []
        s.count = 0
        s.waited = {}
        s.last = None

    def wait(s, tok):
        if tok is None:
            return
        if tok.eng is s and not s.self_sync:
            return
        key = id(tok.sem)
        if s.waited.get(key, 0) >= tok.val:
            return
        s.waited[key] = tok.val
        sem, val = tok.sem, tok.val
        s.ops.append(lambda e: e.wait_ge(sem, val))

    def op(s, fn, signal=True):
        if signal:
            s.count += 1
            sem = s.sem
            s.ops.append(lambda e: fn(e).then_inc(sem, 1))
            s.last = Tok(s.sem, s.count, s)
            return s.last
        s.ops.append(fn)
        return None


class Res:
    __slots__ = ("w", "r")

    def __init__(s):
        s.w = None
        s.r = {}


def _deps(eng, reads, writes):
    for r in reads:
        eng.wait(r.w)
    for w in writes:
        eng.wait(w.w)
        for t in w.r.values():
            eng.wait(t)


def _mark(tok, eng, reads, writes, key=None):
    key = key or eng.name
    for r in reads:
        r.r[key] = tok
    for w in writes:
        w.w = tok
        w.r = {}


def emit(eng, fn, reads=(), writes=()):
    _deps(eng, reads, writes)
    tok = eng.op(fn, True)
    _mark(tok, eng, reads, writes)
    return tok


def op_act(out, in_, func, scale=None, bias=None, accum_out=None):
    kw = {}
    if scale is not None:
        kw["scale"] = scale
    if bias is not None:
        kw["bias"] = bias
    if accum_out is not None:
        kw["accum_out"] = accum_out
    return lambda e: e.activation(out=out, in_=in_, func=func, **kw)


def op_tt(out, in0, in1, op):
    return lambda e: e.tensor_tensor(out=out, in0=in0, in1=in1, op=op)


def op_ts(out, in0, s1, s2, op0, op1=None):
    if op1 is None:
        return lambda e: e.tensor_scalar(out=out, in0=in0, scalar1=s1, scalar2=None, op0=op0)
    return lambda e: e.tensor_scalar(out=out, in0=in0, scalar1=s1, scalar2=s2, op0=op0, op1=op1)


def op_stt(out, in0, scalar, in1, op0, op1):
    return lambda e: e.scalar_tensor_tensor(out=out, in0=in0, scalar=scalar, in1=in1, op0=op0, op1=op1)


def op_copy(out, in_):
    return lambda e: e.tensor_copy(out=out, in_=in_)


def op_memset(out, val):
    return lambda e: e.memset(out, val)


def op_recip(out, in_):
    return lambda e: e.reciprocal(out=out, in_=in_)


def op_mm(out, lhsT, rhs, start, stop):
    return lambda e: e.matmul(out, lhsT, rhs, start=start, stop=stop)


def op_dma(out, in_):
    return lambda e: e.dma_start(out=out, in_=in_)


class Buf:
    def __init__(s, ap):
        s.ap = ap
        s.res = Res()


class Rot:
    def __init__(s, bufs):
        s.bufs = bufs
        s.i = 0

    def next(s):
        b = s.bufs[s.i % len(s.bufs)]
        s.i += 1
        return b


class Arena:
    def __init__(s, ap_f32, nwords):
        s.ap, s.n, s.off = ap_f32, nwords, 0

    def reset(s):
        s.off = 0

    def alloc(s, cols, dtype):
        words = cols if dtype == F32 else (cols + 1) // 2
        words = (words + 7) // 8 * 8
        assert s.off + words <= s.n, ("arena overflow", s.off, words, s.n)
        v = s.ap[:, s.off:s.off + words]
        s.off += words
        if dtype != F32:
            v = v.bitcast(dtype)
        return v[:, 0:cols]


def build_program(cfg):
    nc = bass.Bass("TRN2", target_bir_lowering=False, dynamic_dma_scratch_size=256)
    KD, NF, KC, KP, GP = cfg.KD, cfg.NF, cfg.KC, cfg.KP, cfg.GP
    TA, TB, HALF = cfg.TA, cfg.TB, cfg.HALF
    TM = TA
    order = piece_order(cfg)
    NP = len(order)
    PL = ParLayout(cfg)

    d_x = nc.dram_tensor("xT", [128, KD, HALO + cfg.OWN], F32, kind="ExternalInput").ap()
    d_w = nc.dram_tensor("wst", [NP, 128, PIECE], F32, kind="ExternalInput").ap()
    d_par = nc.dram_tensor("par", [128, PL.n], F32, kind="ExternalInput").ap()
    d_odd = nc.dram_tensor("oddp", [cfg.N_ODD, 128, 2048], F32, kind="ExternalInput").ap()
    d_out = nc.dram_tensor("outT", [128, KD, cfg.OWN], F32, kind="ExternalOutput").ap()

    ARENA_WORDS = 17024
    from contextlib import ExitStack
    with ExitStack() as es:
        def sb(name, shape, dt):
            return es.enter_context(nc.sbuf_tensor(name, shape, dt))

        def sem(name):
            return es.enter_context(nc.semaphore(name))

        x_sb = sb("x_sb", [128, KD, TM], F32)
        h_sb = sb("h_sb", [128, KD, TM], BF16)
        ring = sb("ring", [128, cfg.NR, PIECE], BF16)
        stage = sb("stage", [128, cfg.NS, PIECE], F32)
        actb = sb("actb", [128, 2, cfg.G, TM], BF16)
        par = sb("par_sb", [128, PL.n], F32)
        ones32 = sb("ones32", [128, 128], F32)
        tmp4 = sb("tmp4", [128, 4, 512], F32)
        rstd_sb = sb("rstd", [128, TM], F32)
        ctxA = sb("ctxA", [128, cfg.N_EVEN, KC, 32], F32)
        ctxP = sb("ctxP", [128, cfg.N_EVEN, KP, 16], F32)
        arena_t = sb("arena", [128, ARENA_WORDS], F32)
        psum = [es.enter_context(nc.psum_tensor("ps%d" % i, [128, 512], F32)) for i in range(8)]

        PE = Eng("pe", sem("s_pe"), False)
        ACT = Eng("act", sem("s_act"), True)
        DVE = Eng("dve", sem("s_dve"), True)
        POOL = Eng("pool", sem("s_pool"), True)
        SP = Eng("sp", sem("s_sp"), False)
        stage_sem = [sem("s_st%d" % i) for i in range(cfg.NS)]
        stage_cnt = [0] * cfg.NS
        par_sem, x_sem, odd_sem = sem("s_par"), sem("s_x"), sem("s_odd")
        par_cnt, x_cnt, odd_cnt = [0], [0], [0]
        out_sem = sem("s_out")
        out_cnt = [0]

        def dma(eng, out, in_, semh, cnt, idx, reads=(), writes=()):
            _deps(eng, reads, writes)
            cnt[idx] += 16
            val = cnt[idx]
            eng.ops.append(lambda e: e.dma_start(out=out, in_=in_).then_inc(semh, 16))
            tok = Tok(semh, val, None)
            _mark(tok, eng, reads, writes, key=eng.name + "_dma")
            return tok

        bank = [Buf(psum[i][:]) for i in range(8)]
        rotA = Rot(bank[0:2])
        rotB = Rot(bank[2:4])
        rotD = Rot(bank[4:8])
        bankS0, bankS1 = bank[0], bank[2]
        tmpR = Rot([Buf(tmp4[:, i, :]) for i in range(4)])
        rtR = tmpR
        stage_res = [Res() for _ in range(cfg.NS)]
        ring_res = [Res() for _ in range(cfg.NR)]
        NSEG = 1 + (HALF + 511) // 512
        x_res = [[Res() for _ in range(KD)] for _ in range(NSEG)]
        h_res = [[Res() for _ in range(KD)] for _ in range(NSEG)]
        rstd_res = [Res() for _ in range(NSEG)]
        act_res = [[[Res() for _ in range(NSEG)] for _ in range(cfg.G)] for _ in range(2)]
        par_res = Res()
        ones_res = Res()
        ctxA_res = [[Res() for _ in range(KC)] for _ in range(cfg.N_EVEN)]
        ctxP_res = [[Res() for _ in range(KP)] for _ in range(cfg.N_EVEN)]
        arena = Arena(arena_t[:], ARENA_WORDS)

        def pcol(name, j=0, w=1):
            o, _ = PL.off[name]
            return par[:, o + j:o + j + w]

        def tiles(hf, lo):
            T = TA if hf == 0 else TB
            out = []
            if hf == 0:
                if lo < HALO:
                    out.append((0, lo, HALO - lo))
                c = HALO
            else:
                c = 0
            s = 1
            while c < T:
                n = min(512, T - c)
                out.append((s, c, n))
                c += n
                s += 1
            return out

        class WS:
            LOOK = 5

            def __init__(s):
                s.dma_emitted = 0
                s.cast_emitted = 0
                s.fetched = 0
                s.released = [False] * (2 * NP)

            def _dma(s, q):
                ss = q % cfg.NS
                dma(SP, stage[:, ss, :], d_w[q % NP], stage_sem[ss], stage_cnt, ss,
                    writes=[stage_res[ss]])
                s.dma_emitted += 1

            def advance(s):
                while s.cast_emitted < 2 * NP and s.cast_emitted < s.fetched + s.LOOK:
                    q = s.cast_emitted
                    if q >= cfg.NR and not s.released[q - cfg.NR]:
                        break
                    if s.dma_emitted <= q:
                        s._dma(q)
                    ss, rs = q % cfg.NS, q % cfg.NR
                    if False:
                        emit(POOL, op_copy(ring[:, rs, :], stage[:, ss, :]),
                             reads=[stage_res[ss]], writes=[ring_res[rs]])
                    else:
                        emit(ACT, op_act(ring[:, rs, :], stage[:, ss, :], AF.Copy),
                             reads=[stage_res[ss]], writes=[ring_res[rs]])
                    s.cast_emitted += 1
                    while s.dma_emitted < min(2 * NP, s.cast_emitted + cfg.NS):
                        s._dma(s.dma_emitted)

            def next(s, tag):
                q = s.fetched
                assert order[q % NP] == tag, (q, order[q % NP], tag)
                s.fetched += 1
                s.advance()
                assert s.cast_emitted > q, "weight ring too small (build-time deadlock) at %s" % (tag,)
                rs = q % cfg.NR
                return q, ring[:, rs, :], ring_res[rs]

            def release(s, q):
                s.released[q] = True
                s.advance()

        ws = WS()

        def mm_group(out_ap, pairs, reads, writes, step_reads=None):
            reads = list(reads) + list(step_reads or [])
            _deps(PE, reads, writes)
            n = len(pairs)
            tok = None
            for i, (l, r) in enumerate(pairs):
                tok = PE.op(op_mm(out_ap, l, r, i == 0, i == n - 1), signal=(i == n - 1))
            _mark(tok, PE, reads, writes)
            return tok

        def mm_step(ps, lhsT, rhs, first, last, reads):
            _deps(PE, reads, [ps.res] if first else [])
            tok = PE.op(op_mm(ps.ap[:, :rhs.shape[-1]], lhsT, rhs, first, last), True)
            for r in reads:
                r.r["pe"] = tok
            if first:
                ps.res.r = {}
            ps.res.w = tok
            return tok

        def barrier():
            toks = [PE.last, ACT.last, DVE.last]
            for e in (PE, ACT, DVE):
                for t in toks:
                    e.wait(t)

        def rmsnorm(hf, lo, gname, inplace=False, deferred=False):
            for (sg, c0, n) in tiles(hf, lo):
                ps = bankS0
                for k in range(KD):
                    sq = tmpR.next()
                    emit(ACT, op_act(sq.ap[:, :n], x_sb[:, k, c0:c0 + n], AF.Square),
                         reads=[x_res[sg][k]], writes=[sq.res])
                    if deferred:
                        emit(POOL, op_ts(h_sb[:, k, c0:c0 + n], x_sb[:, k, c0:c0 + n], pcol(gname, k), 1.0,
                                         ALU.mult, ALU.mult),
                             reads=[x_res[sg][k], par_res], writes=[h_res[sg][k]])
                    mm_step(ps, ones32[:], sq.ap[:, :n], k == 0, k == KD - 1, [sq.res, ones_res])
                rt = rtR.next()
                emit(ACT, op_act(rt.ap[:, :n], ps.ap[:, :n], AF.Sqrt, scale=1.0 / cfg.D, bias=EPS),
                     reads=[ps.res], writes=[rt.res])
                emit(DVE, op_recip(rstd_sb[:, c0:c0 + n], rt.ap[:, :n]),
                     reads=[rt.res], writes=[rstd_res[sg]])
                for k in range(KD):
                    if deferred:
                        break
                    if inplace:
                        emit(DVE, op_stt(x_sb[:, k, c0:c0 + n], x_sb[:, k, c0:c0 + n], pcol(gname, k),
                                         rstd_sb[:, c0:c0 + n], ALU.mult, ALU.mult),
                             reads=[rstd_res[sg], par_res], writes=[x_res[sg][k]])
                    else:
                        emit(DVE, op_stt(h_sb[:, k, c0:c0 + n], x_sb[:, k, c0:c0 + n], pcol(gname, k),
                                         rstd_sb[:, c0:c0 + n], ALU.mult, ALU.mult),
                             reads=[x_res[sg][k], rstd_res[sg], par_res], writes=[h_res[sg][k]])

        def proj_group(hf, lo, wl, src_fn, src_res_fn, scale):
            tl = tiles(hf, lo)
            for dm in range(KD):
                for (sg, c0, n) in tl:
                    pd = rotD.next()
                    pairs = [(w[1][:, dm * 128:(dm + 1) * 128], src_fn(i, c0, n)) for i, w in enumerate(wl)]
                    rd = [w[2] for w in wl] + [src_res_fn(i, sg) for i in range(len(wl))]
                    mm_group(pd.ap[:, :n], pairs, rd, [pd.res])
                    emit(DVE, op_stt(x_sb[:, dm, c0:c0 + n], pd.ap[:, :n], float(scale),
                                     x_sb[:, dm, c0:c0 + n], ALU.mult, ALU.add),
                         reads=[pd.res], writes=[x_res[sg][dm]])
            for w in wl:
                ws.release(w[0])

        def ffn(hf, l, w, lo):
            rmsnorm(hf, lo, ("norm", l, 0 if w == 1 else 2), deferred=True)
            groups = ffn_groups(cfg)
            tl = tiles(hf, lo)

            def GU(g):
                b = g % 2
                for fi, f in enumerate(groups[g]):
                    qg, wg, rg = ws.next(("ffn", l, w, "g", f))
                    qu, wu, ru = ws.next(("ffn", l, w, "u", f))
                    for (sg, c0, n) in tl:
                        pg, pu = rotA.next(), rotB.next()
                        mm_group(pg.ap[:, :n], [(wg[:, k * 128:(k + 1) * 128], h_sb[:, k, c0:c0 + n]) for k in range(KD)],
                                 [rg], [pg.res], step_reads=h_res[sg])
                        mm_group(pu.ap[:, :n], [(wu[:, k * 128:(k + 1) * 128], h_sb[:, k, c0:c0 + n]) for k in range(KD)],
                                 [ru], [pu.res], step_reads=h_res[sg])
                        sl, t2 = tmpR.next(), tmpR.next()
                        rs_ap = rstd_sb[:, c0:c0 + n]
                        emit(DVE, op_tt(sl.ap[:, :n], pg.ap[:, :n], rs_ap, ALU.mult),
                             reads=[pg.res, rstd_res[sg]], writes=[sl.res])
                        emit(ACT, op_act(sl.ap[:, :n], sl.ap[:, :n], AF.Silu), reads=[sl.res], writes=[sl.res])
                        emit(DVE, op_tt(t2.ap[:, :n], pu.ap[:, :n], rs_ap, ALU.mult),
                             reads=[pu.res, rstd_res[sg]], writes=[t2.res])
                        emit(DVE, op_tt(actb[:, b, fi, c0:c0 + n], sl.ap[:, :n], t2.ap[:, :n], ALU.mult),
                             reads=[sl.res, t2.res], writes=[act_res[b][fi][sg]])
                    ws.release(qg)
                    ws.release(qu)

            def DN(g):
                b = g % 2
                wl = [ws.next(("ffn", l, w, "d", f)) for f in groups[g]]
                proj_group(hf, lo, wl, lambda i, c0, n: actb[:, b, i, c0:c0 + n],
                           lambda i, sg: act_res[b][i][sg], 0.5)

            GU(0)
            for g in range(1, len(groups)):
                GU(g)
                DN(g - 1)
            DN(len(groups) - 1)

        def out_proj(hf, lo, tagfn, nK, src_ap_fn, src_res_fn, scale, GK=4, korder=None):
            korder = list(range(nK)) if korder is None else korder
            for k0 in range(0, nK, GK):
                ks = korder[k0:k0 + GK]
                wl = [ws.next(tagfn(kc)) for kc in ks]
                proj_group(hf, lo, wl, lambda i, c0, n: src_ap_fn(ks[i], c0, n),
                           lambda i, sg: src_res_fn(ks[i], sg), scale)

        def mixer_even(hf, l, lo_in, lo_out):
            i = l // 2
            T = TA if hf == 0 else TB
            rmsnorm(hf, lo_in, ("norm", l, 1), deferred=True)
            barrier()
            arena.reset()
            AG = [Buf(arena.alloc(32 + TM, BF16)) for _ in range(2)]
            DGS = [Buf(arena.alloc(128, BF16)) for _ in range(CONV_W)]
            Y = arena.alloc(KC * TM, F32).rearrange("p (k t) -> p k t", k=KC)
            Y_res = [[Res() for _ in range(NSEG)] for _ in range(KC)]
            PB = [Buf(arena.alloc(16 + TM, F32)) for _ in range(2)]
            PT = [Buf(arena.alloc(16 + TM, F32)) for _ in range(2)]
            PLD = arena.alloc(GP * TM, BF16).rearrange("p (k t) -> p k t", k=GP)
            PLD_res = [Res() for _ in range(GP)]
            CATB = arena.alloc(KP * TM, BF16).rearrange("p (k t) -> p k t", k=KP)
            CATB_res = [[Res() for _ in range(NSEG)] for _ in range(KP)]

            def cat_ap(kc, c0, n):
                return h_sb[:, kc, c0:c0 + n] if kc < KC else CATB[:, kc - KC, c0:c0 + n]

            def cat_res(kc, sg):
                return h_res[sg][kc] if kc < KC else CATB_res[kc - KC][sg]
            MEAN = arena.alloc(TM, F32)
            RSTD = arena.alloc(TM, F32)
            M2 = arena.alloc(TM, F32)
            st_res = [Res() for _ in range(NSEG)]
            tl_in = tiles(hf, lo_in)
            tl_out = tiles(hf, lo_out)
            nout = T - lo_out
            def gen_diags(j):
                o, _ = PL.off[("convw", i)]
                for k in range(CONV_W):
                    wk = par[:, o + j * CONV_W + k:o + j * CONV_W + k + 1]
                    emit(DVE, op_ts(DGS[k].ap, pcol(("ident",), 0, 128), wk, None, ALU.mult),
                         reads=[par_res], writes=[DGS[k].res])

            def conv(j):
                ag = AG[j % 2]
                pcs = [rotD.next() for _ in tl_out]
                tok = None
                for k in range(CONV_W):
                    dg = DGS[k]
                    _deps(PE, [dg.res, ag.res], [pc.res for pc in pcs] if k == 0 else [])
                    for ti, (sg, c0, n) in enumerate(tl_out):
                        last = (ti == len(tl_out) - 1)
                        tok = PE.op(op_mm(pcs[ti].ap[:, :n], dg.ap, ag.ap[:, 2 + c0 + k:2 + c0 + k + n],
                                          k == 0, k == CONV_W - 1), signal=last)
                    dg.res.r["pe"] = tok
                ag.res.r["pe"] = tok
                for ti, (sg, c0, n) in enumerate(tl_out):
                    pcs[ti].res.w = tok
                    pcs[ti].res.r = {}
                    emit(DVE, op_ts(Y[:, j, c0:c0 + n], pcs[ti].ap[:, :n], pcol(("convb", i), j), None, ALU.add),
                         reads=[pcs[ti].res, par_res], writes=[Y_res[j][sg]])

            for j in range(KC):
                if j > 0:
                    gen_diags(j - 1)
                qa, wa, ra = ws.next(("abin", i, j))
                qg, wg, rg = ws.next(("abin", i, KC + j))
                ag = AG[j % 2]
                if hf == 0:
                    emit(DVE, op_memset(ag.ap[:, 0:32], 0.0), writes=[ag.res])
                else:
                    emit(DVE, op_copy(ag.ap[:, 0:32], ctxA[:, i, j, :]), reads=[ctxA_res[i][j]], writes=[ag.res])
                for (sg, c0, n) in tl_in:
                    pa, pg = rotA.next(), rotB.next()
                    mm_group(pa.ap[:, :n], [(wa[:, k * 128:(k + 1) * 128], h_sb[:, k, c0:c0 + n]) for k in range(KD)],
                             [ra], [pa.res], step_reads=h_res[sg])
                    mm_group(pg.ap[:, :n], [(wg[:, k * 128:(k + 1) * 128], h_sb[:, k, c0:c0 + n]) for k in range(KD)],
                             [rg], [pg.res], step_reads=h_res[sg])
                    sg_t, a_t = tmpR.next(), tmpR.next()
                    rs_ap = rstd_sb[:, c0:c0 + n]
                    emit(DVE, op_tt(sg_t.ap[:, :n], pg.ap[:, :n], rs_ap, ALU.mult),
                         reads=[pg.res, rstd_res[sg]], writes=[sg_t.res])
                    emit(ACT, op_act(sg_t.ap[:, :n], sg_t.ap[:, :n], AF.Sigmoid), reads=[sg_t.res], writes=[sg_t.res])
                    emit(DVE, op_tt(a_t.ap[:, :n], pa.ap[:, :n], rs_ap, ALU.mult),
                         reads=[pa.res, rstd_res[sg]], writes=[a_t.res])
                    emit(DVE, op_tt(ag.ap[:, 32 + c0:32 + c0 + n], sg_t.ap[:, :n], a_t.ap[:, :n], ALU.mult),
                         reads=[sg_t.res, a_t.res], writes=[ag.res])
                    if hf == 0 and c0 < HALO:
                        emit(DVE, op_tt(ag.ap[:, 32 + c0:32 + HALO], ag.ap[:, 32 + c0:32 + HALO],
                                        pcol(("hmask",), c0, HALO - c0), ALU.mult),
                             reads=[par_res], writes=[ag.res])
                ws.release(qa)
                ws.release(qg)
                if hf == 0:
                    emit(ACT, op_act(ctxA[:, i, j, :], ag.ap[:, 32 + T - 32:32 + T], AF.Copy),
                         reads=[ag.res], writes=[ctxA_res[i][j]])
                if j > 0:
                    conv(j - 1)
            gen_diags(KC - 1)
            conv(KC - 1)
            wpool = None
            for grp in range(4):
                win = 2 << grp
                for jj in range(GP):
                    j = grp * GP + jj
                    qp, wp, rp = ws.next(("abin", i, 2 * KC + j))
                    pb = PB[j % 2]
                    if hf == 0:
                        emit(DVE, op_memset(pb.ap[:, 0:16], 0.0), writes=[pb.res])
                    else:
                        emit(DVE, op_copy(pb.ap[:, 0:16], ctxP[:, i, j, :]), reads=[ctxP_res[i][j]], writes=[pb.res])
                    for (sg, c0, n) in tl_in:
                        pp = rotA.next()
                        mm_group(pp.ap[:, :n], [(wp[:, k * 128:(k + 1) * 128], h_sb[:, k, c0:c0 + n]) for k in range(KD)],
                                 [rp] + h_res[sg], [pp.res])
                        emit(DVE, op_tt(pb.ap[:, 16 + c0:16 + c0 + n], pp.ap[:, :n], rstd_sb[:, c0:c0 + n], ALU.mult),
                             reads=[pp.res, rstd_res[sg]], writes=[pb.res])
                        if hf == 0 and c0 < HALO:
                            emit(DVE, op_tt(pb.ap[:, 16 + c0:16 + HALO], pb.ap[:, 16 + c0:16 + HALO],
                                            pcol(("hmask",), c0, HALO - c0), ALU.mult),
                                 reads=[par_res], writes=[pb.res])
                    ws.release(qp)
                    if hf == 0:
                        emit(ACT, op_act(ctxP[:, i, j, :], pb.ap[:, 16 + T - 16:16 + T], AF.Copy),
                             reads=[pb.res], writes=[ctxP_res[i][j]])
                    cur = pb
                    lo_b = (16 + lo_in) if hf == 0 else 0
                    s = 1
                    lvl = 0
                    while s < win:
                        nxt = PT[lvl % 2]
                        a0 = lo_b + s
                        emit(DVE, op_tt(nxt.ap[:, a0:16 + T], cur.ap[:, a0:16 + T], cur.ap[:, a0 - s:16 + T - s], ALU.add),
                             reads=[cur.res], writes=[nxt.res])
                        cur = nxt
                        lo_b = a0
                        s *= 2
                        lvl += 1
                    if hf == 0:
                        o, _ = PL.off[("pcorr",)]
                        emit(DVE, op_tt(cur.ap[:, 16 + HALO:16 + HALO + 16], cur.ap[:, 16 + HALO:16 + HALO + 16],
                                        par[:, o + grp * 16:o + grp * 16 + 16], ALU.mult),
                             reads=[par_res], writes=[cur.res])
                    emit(DVE, op_stt(PLD[:, jj, lo_out:T], cur.ap[:, 16 + lo_out:16 + T], 1.0 / win,
                                     pb.ap[:, 16 + lo_out:16 + T], ALU.mult, ALU.subtract),
                         reads=[cur.res, pb.res], writes=[PLD_res[jj]])
                if grp == 0:
                    wpool = ws.next(("poolw", i))
                PG = GP * 128
                for dj in range(GP):
                    for (sg, c0, n) in tl_out:
                        pq = rotB.next()
                        pairs = []
                        for kc in range(GP):
                            base = (grp * GP + kc) * PG + dj * 128
                            pairs.append((wpool[1][:, base:base + 128], PLD[:, kc, c0:c0 + n]))
                        mm_group(pq.ap[:, :n], pairs, [wpool[2]] + PLD_res, [pq.res])
                        emit(DVE, op_ts(CATB[:, grp * GP + dj, c0:c0 + n], pq.ap[:, :n],
                                        pcol(("pscale", i), grp * GP + dj), None, ALU.mult),
                             reads=[pq.res, par_res], writes=[CATB_res[grp * GP + dj][sg]])
            ws.release(wpool[0])
            for (sg, c0, n) in tl_out:
                p1, p2 = bankS0, bankS1
                for j in range(KC):
                    mm_step(p1, ones32[:], Y[:, j, c0:c0 + n], j == 0, j == KC - 1, [Y_res[j][sg], ones_res])
                for j in range(KC):
                    sq = tmpR.next()
                    emit(ACT, op_act(sq.ap[:, :n], Y[:, j, c0:c0 + n], AF.Square), reads=[Y_res[j][sg]], writes=[sq.res])
                    mm_step(p2, ones32[:], sq.ap[:, :n], j == 0, j == KC - 1, [sq.res, ones_res])
                DC = float(KC * 128)
                emit(DVE, op_ts(MEAN[:, c0:c0 + n], p1.ap[:, :n], 1.0 / DC, None, ALU.mult),
                     reads=[p1.res], writes=[st_res[sg]])
                emit(DVE, op_tt(M2[:, c0:c0 + n], MEAN[:, c0:c0 + n], MEAN[:, c0:c0 + n], ALU.mult),
                     reads=[st_res[sg]], writes=[st_res[sg]])
                emit(DVE, op_stt(M2[:, c0:c0 + n], p2.ap[:, :n], 1.0 / DC, M2[:, c0:c0 + n], ALU.mult, ALU.subtract),
                     reads=[p2.res], writes=[st_res[sg]])
                rt = rtR.next()
                emit(ACT, op_act(rt.ap[:, :n], M2[:, c0:c0 + n], AF.Sqrt, bias=EPS), reads=[st_res[sg]], writes=[rt.res])
                emit(DVE, op_recip(RSTD[:, c0:c0 + n], rt.ap[:, :n]), reads=[rt.res], writes=[st_res[sg]])
                for j in range(KC):
                    emit(DVE, op_tt(Y[:, j, c0:c0 + n], Y[:, j, c0:c0 + n], MEAN[:, c0:c0 + n], ALU.subtract),
                         reads=[st_res[sg]], writes=[Y_res[j][sg]])
                    emit(DVE, op_tt(Y[:, j, c0:c0 + n], Y[:, j, c0:c0 + n], RSTD[:, c0:c0 + n], ALU.mult),
                         reads=[st_res[sg]], writes=[Y_res[j][sg]])
                    emit(ACT, op_act(h_sb[:, j, c0:c0 + n], Y[:, j, c0:c0 + n], AF.Silu,
                                     scale=pcol(("clng", i), j), bias=pcol(("clnb", i), j)),
                         reads=[Y_res[j][sg], par_res], writes=[h_res[sg][j]])
            out_proj(hf, lo_out, lambda kc: ("about", i, kc), 2 * KC,
                     cat_ap, cat_res, 1.0, korder=about_korder(cfg))
            barrier()

        def mixer_odd(hf, l, lo_in, lo_out):
            i = l // 2
            T = TA if hf == 0 else TB
            rmsnorm(hf, lo_in, ("norm", l, 1))
            barrier()
            arena.reset()
            nW = (T - lo_in) // 128
            assert nW * 128 == T - lo_in
            D = cfg.D
            TW_ = T - lo_in
            ODDP = Buf(arena.alloc(2048, F32))
            WT = ODDP.ap[:, 0:1024].rearrange("p (g i) -> p g i", g=8)
            SGB = ODDP.ap[:, 1024:2048].rearrange("p (g i) -> p g i", g=8)
            WTb = Buf(arena.alloc(1024, BF16))
            WTb3 = WTb.ap.rearrange("p (g i) -> p g i", g=8)
            RS = Buf(arena.alloc(1024, F32))
            RS3 = RS.ap.rearrange("p (g i) -> p g i", g=8)
            VN = arena.alloc(nW * D, BF16).rearrange("p (w d) -> p w d", w=nW)
            VN_res = [Res() for _ in range(nW)]
            UB = [Buf(arena.alloc(TW_, F32)) for _ in range(2)]
            TWB = [Buf(arena.alloc(128, F32)) for _ in range(2)]
            B2 = [Buf(arena.alloc(128, F32)) for _ in range(2)]
            JK = Buf(arena.alloc(128, F32))
            S1 = Buf(arena.alloc(nW * KD, F32))
            S2 = Buf(arena.alloc(nW * KD, F32))
            S13 = S1.ap.rearrange("p (w c) -> p w c", w=nW)
            S23 = S2.ap.rearrange("p (w c) -> p w c", w=nW)
            MR = Buf(arena.alloc(8 * nW, F32))
            s1r, s2r = MR.ap[:, 0:nW], MR.ap[:, nW:2 * nW]
            mean, m2 = MR.ap[:, 2 * nW:3 * nW], MR.ap[:, 3 * nW:4 * nW]
            var, rsd = MR.ap[:, 4 * nW:5 * nW], MR.ap[:, 5 * nW:6 * nW]
            GT = Rot([Buf(arena.alloc(512, F32)) for _ in range(2)])
            Gb = arena.alloc(KD * TW_, BF16).rearrange("p (k t) -> p k t", k=KD)
            G_res = [[Res() for _ in range(NSEG)] for _ in range(KD)]

            dma(ACT, ODDP.ap, d_odd[i], odd_sem, odd_cnt, 0, writes=[ODDP.res])
            emit(DVE, op_memset(WT[64:128, :, 0:64], 0.0), writes=[ODDP.res])
            emit(DVE, op_copy(WTb.ap, ODDP.ap[:, 0:1024]), reads=[ODDP.res], writes=[WTb.res])
            for hh in range(2):
                pr = rotD.next()
                _deps(PE, [ODDP.res, ones_res], [pr.res])
                tok = PE.op(op_mm(pr.ap[:, :512], ones32[:], ODDP.ap[:, hh * 512:(hh + 1) * 512], True, True), True)
                _mark(tok, PE, [ODDP.res], [pr.res])
                emit(ACT, op_act(RS.ap[:, hh * 512:(hh + 1) * 512], pr.ap[:, :512], AF.Copy), reads=[pr.res], writes=[RS.res])

            wins = list(range(nW))
            allh = [r for sg in range(NSEG) for r in h_res[sg]]

            def vmm(c, wv, rv, wb):
                pv = rotD.next()
                _deps(PE, [rv] + allh, [pv.res])
                tok = None
                for wi, w in enumerate(wb):
                    t0 = lo_in + w * 128
                    for k in range(KD):
                        tok = PE.op(op_mm(pv.ap[:, wi * 128:(wi + 1) * 128], h_sb[:, k, t0:t0 + 128],
                                          wv[:, k * 128:(k + 1) * 128], k == 0, k == KD - 1),
                                    signal=(wi == len(wb) - 1 and k == KD - 1))
                _mark(tok, PE, [rv] + allh, [pv.res])
                gt = GT.next()
                nb = len(wb)
                emit(ACT, op_act(gt.ap[:, :nb * 128], pv.ap[:, :nb * 128], AF.Gelu_apprx_tanh),
                     reads=[pv.res], writes=[gt.res])
                return gt

            for c in range(KD):
                qv, wv, rv = ws.next(("cin", i, "v1", c))
                for w0 in range(0, nW, 4):
                    wb = wins[w0:w0 + 4]
                    gt = vmm(c, wv, rv, wb)
                    for wi, w in enumerate(wb):
                        gsl = gt.ap[:, wi * 128:(wi + 1) * 128]
                        s1c, s2c = S13[:, w, c:c + 1], S23[:, w, c:c + 1]
                        emit(DVE, lambda e, gsl=gsl, s1c=s1c: e.tensor_scalar(
                                 out=JK.ap, in0=gsl, scalar1=1.0, scalar2=None, op0=ALU.mult, op1=ALU.add, accum_out=s1c),
                             reads=[gt.res], writes=[JK.res, S1.res])
                        emit(DVE, lambda e, gsl=gsl, s2c=s2c: e.scalar_tensor_tensor(
                                 out=JK.ap, in0=gsl, scalar=1.0, in1=gsl, op0=ALU.mult, op1=ALU.mult, accum_out=s2c),
                             reads=[gt.res], writes=[JK.res, S2.res])
                ws.release(qv)
            for w in range(nW):
                emit(ACT, op_act(JK.ap[:, 0:KD], S13[:, w, :], AF.Copy, accum_out=s1r[:, w:w + 1]),
                     reads=[S1.res], writes=[JK.res, MR.res])
                emit(ACT, op_act(JK.ap[:, 0:KD], S23[:, w, :], AF.Copy, accum_out=s2r[:, w:w + 1]),
                     reads=[S2.res], writes=[JK.res, MR.res])
            emit(DVE, op_ts(mean, s1r, 1.0 / D, None, ALU.mult), reads=[MR.res], writes=[MR.res])
            emit(DVE, op_tt(m2, mean, mean, ALU.mult), reads=[MR.res], writes=[MR.res])
            emit(DVE, op_stt(var, s2r, 1.0 / D, m2, ALU.mult, ALU.subtract), reads=[MR.res], writes=[MR.res])
            emit(ACT, op_act(var, var, AF.Sqrt, bias=EPS), reads=[MR.res], writes=[MR.res])
            emit(DVE, op_recip(rsd, var), reads=[MR.res], writes=[MR.res])
            for c in range(KD):
                qv, wv, rv = ws.next(("cin", i, "v", c))
                for w0 in range(0, nW, 4):
                    wb = wins[w0:w0 + 4]
                    gt = vmm(c, wv, rv, wb)
                    for wi, w in enumerate(wb):
                        emit(DVE, op_ts(VN[:, w, c * 128:(c + 1) * 128], gt.ap[:, wi * 128:(wi + 1) * 128],
                                        mean[:, w:w + 1], rsd[:, w:w + 1], ALU.subtract, ALU.mult),
                             reads=[gt.res, MR.res], writes=[VN_res[w]])
                ws.release(qv)
            tl_in = tiles(hf, lo_in)
            for c in range(KD):
                g = c // cfg.HC
                qu, wu, ru = ws.next(("cin", i, "u", c))
                ub = UB[c % 2]
                b2 = B2[c % 2]
                emit(DVE, op_stt(b2.ap, RS3[:, g, :], pcol(("slnb", i), c), SGB[:, g, :], ALU.mult, ALU.add),
                     reads=[RS.res, ODDP.res, par_res], writes=[b2.res])
                for (sg, c0, n) in tl_in:
                    pu = rotB.next()
                    mm_group(pu.ap[:, :n], [(wu[:, k * 128:(k + 1) * 128], h_sb[:, k, c0:c0 + n]) for k in range(KD)],
                             [ru] + h_res[sg], [pu.res])
                    emit(ACT, op_act(ub.ap[:, c0 - lo_in:c0 - lo_in + n], pu.ap[:, :n], AF.Gelu_apprx_tanh),
                         reads=[pu.res], writes=[ub.res])
                ws.release(qu)
                for w0 in range(0, nW, 4):
                    wb = wins[w0:w0 + 4]
                    pm = rotA.next()
                    rd = [VN_res[w] for w in wb] + [WTb.res]
                    _deps(PE, rd, [pm.res])
                    tok = None
                    for wi, w in enumerate(wb):
                        tok = PE.op(op_mm(pm.ap[:, wi * 128:(wi + 1) * 128], VN[:, w, c * 128:(c + 1) * 128],
                                          WTb3[:, g, :], True, True), signal=(wi == len(wb) - 1))
                    _mark(tok, PE, rd, [pm.res])
                    for wi, w in enumerate(wb):
                        t0 = lo_in + w * 128
                        tw = TWB[w % 2]
                        emit(DVE, op_stt(tw.ap, pm.ap[:, wi * 128:(wi + 1) * 128], pcol(("slng", i), c),
                                         b2.ap, ALU.mult, ALU.add),
                             reads=[pm.res, b2.res, par_res], writes=[tw.res])
                        sgs = [sg for (sg, c0, n) in tl_in if c0 < t0 + 128 and t0 < c0 + n]
                        emit(DVE, op_tt(Gb[:, c, t0 - lo_in:t0 - lo_in + 128], tw.ap, ub.ap[:, t0 - lo_in:t0 - lo_in + 128], ALU.mult),
                             reads=[tw.res, ub.res], writes=[G_res[c][sg] for sg in sgs])
            out_proj(hf, lo_out, lambda kc: ("cout", i, kc), KD,
                     lambda kc, c0, n: Gb[:, kc, c0 - lo_in:c0 - lo_in + n], lambda kc, sg: G_res[kc][sg], 1.0)
            barrier()

        emit(DVE, op_memset(ones32[:], 1.0), writes=[ones_res])
        dma(SP, par[:], d_par, par_sem, par_cnt, 0, writes=[par_res])
        LO_IN = [0, 32, 128, 160]
        LO_OUT = [32, 128, 160, 160]
        for hf in range(2):
            T = TA if hf == 0 else TB
            off = 0 if hf == 0 else TA
            allx = [r for sg in range(NSEG) for r in x_res[sg]]
            dma(SP, x_sb[:, :, 0:T], d_x[:, :, off:off + T], x_sem, x_cnt, 0, writes=allx)
            for l in range(cfg.DEPTH):
                li = LO_IN[l] if hf == 0 else 0
                lo = LO_OUT[l] if hf == 0 else 0
                ffn(hf, l, 1, li)
                if l % 2 == 0:
                    mixer_even(hf, l, li, lo)
                else:
                    mixer_odd(hf, l, li, lo)
                ffn(hf, l, 2, lo)
            lo = HALO if hf == 0 else 0
            rmsnorm(hf, lo, ("fnorm",), inplace=True)
            ooff = 0 if hf == 0 else HALF
            dma(ACT, d_out[:, :, ooff:ooff + HALF], x_sb[:, :, lo:lo + HALF], out_sem, out_cnt, 0, reads=allx)
        assert ws.fetched == 2 * NP, (ws.fetched, NP)
        fin = Tok(out_sem, out_cnt[0], None)
        ACT.wait(fin)

        engs = {"tensor": PE, "scalar": ACT, "vector": DVE, "gpsimd": POOL, "sync": SP}
        with nc.Block() as block:
            for name, E in engs.items():
                def body(e, E=E):
                    for f in E.ops:
                        f(e)
                getattr(block, name)(body)
        stats = {k: len(v.ops) for k, v in engs.items()}
    return nc, stats


_CACHE = {}


def run(cfg, inputs, trace=False):
    inp = {k: np.asarray(v) for k, v in inputs.items()}
    wst = pack_weights(cfg, inp)
    oddp = pack_oddp(cfg, inp)
    in_maps = []
    for c in range(cfg.NCORES):
        in_maps.append({"xT": pack_x(cfg, inp["x"], c), "wst": wst,
                        "par": pack_params(cfg, inp, c), "oddp": oddp})
    nc, stats = build_program(cfg)
    res = run_bass_kernel_spmd(nc, in_maps, core_ids=list(range(cfg.NCORES)), trace=trace)
    out = np.zeros((cfg.BATCH, cfg.SEQ, cfg.D), np.float32)
    for c in range(cfg.NCORES):
        o = res.results[c]["outT"]
        b = c // cfg.CPS
        s0 = (c % cfg.CPS) * cfg.OWN
        out[b, s0:s0 + cfg.OWN, :] = o.transpose(2, 1, 0).reshape(cfg.OWN, cfg.D)
    return out, res, stats


def kernel(**inputs):
    cfg = Cfg()
    out, _, _ = run(cfg, inputs)
    return out
```

```python
import numpy as np
import concourse.bass as bass
import concourse.mybir as mybir
from concourse.bass_utils import run_bass_kernel_spmd

F32 = mybir.dt.float32
BF16 = mybir.dt.bfloat16
AF = mybir.ActivationFunctionType
ALU = mybir.AluOpType

EPS = 1e-6
CONV_W = 31
PIECE = 2048
HALO = 160


class Cfg:
    def __init__(s, D=2048, DFF=5504, SEQ=4096, BATCH=2, DEPTH=4, NCORES=8,
                 G=4, NR=10, NS=3):
        s.D, s.DFF, s.SEQ, s.BATCH, s.DEPTH, s.NCORES = D, DFF, SEQ, BATCH, DEPTH, NCORES
        s.KD = D // 128
        s.NF = DFF // 128
        s.CPS = NCORES // BATCH
        s.OWN = SEQ // s.CPS
        s.HALF = s.OWN // 2
        s.TA = HALO + s.HALF
        s.TB = s.HALF
        s.KC = (D // 2) // 128
        s.KP = (D // 2) // 128
        s.GP = s.KP // 4
        s.HC = s.KD // 8
        s.G = G
        s.NR, s.NS = NR, NS
        s.N_EVEN = (DEPTH + 1) // 2
        s.N_ODD = DEPTH // 2
        assert s.KD * 128 <= PIECE and D <= PIECE and s.HALF % 128 == 0
        assert DEPTH == 4


def ffn_groups(cfg):
    sizes = [cfg.G] * (cfg.NF // cfg.G)
    rem = cfg.NF % cfg.G
    if rem:
        sizes.append(rem)
    if len(sizes) >= 2 and sizes[-1] == 1 and cfg.G >= 3:
        sizes[-2:] = [cfg.G - 1, 2]
    out, i = [], 0
    for n in sizes:
        out.append(list(range(i, i + n)))
        i += n
    assert i == cfg.NF
    return out


def ffn_order(cfg, l, w):
    groups = ffn_groups(cfg)
    seq = []

    def GU(g):
        for f in groups[g]:
            seq.append(("ffn", l, w, "g", f))
            seq.append(("ffn", l, w, "u", f))

    def DN(g):
        for f in groups[g]:
            seq.append(("ffn", l, w, "d", f))

    GU(0)
    for g in range(1, len(groups)):
        GU(g)
        DN(g - 1)
    DN(len(groups) - 1)
    return seq


def about_korder(cfg):
    return list(range(cfg.KC, 2 * cfg.KC)) + list(range(cfg.KC))


def mixer_order(cfg, l):
    i = l // 2
    seq = []
    if l % 2 == 0:
        for j in range(cfg.KC):
            seq.append(("abin", i, j))
            seq.append(("abin", i, cfg.KC + j))
        for grp in range(4):
            for jj in range(cfg.GP):
                seq.append(("abin", i, 2 * cfg.KC + grp * cfg.GP + jj))
            if grp == 0:
                seq.append(("poolw", i))
        for kc in about_korder(cfg):
            seq.append(("about", i, kc))
    else:
        for c in range(cfg.KD):
            seq.append(("cin", i, "v1", c))
        for c in range(cfg.KD):
            seq.append(("cin", i, "v", c))
        for c in range(cfg.KD):
            seq.append(("cin", i, "u", c))
        for kc in range(cfg.KD):
            seq.append(("cout", i, kc))
    return seq


def piece_order(cfg):
    seq = []
    for l in range(cfg.DEPTH):
        seq += ffn_order(cfg, l, 1)
        seq += mixer_order(cfg, l)
        seq += ffn_order(cfg, l, 2)
    return seq


def _colchunk(W, c, KD):
    blk = W[:, c * 128:(c + 1) * 128]
    return blk.reshape(KD, 128, 128).transpose(1, 0, 2).reshape(128, KD * 128)


def pack_weights(cfg, inp):
    order = piece_order(cfg)
    out = np.zeros((len(order), 128, PIECE), np.float32)
    KD = cfg.KD
    for n, tag in enumerate(order):
        if tag[0] == "ffn":
            _, l, w, kind, f = tag
            if kind == "g":
                W = inp["ffn1_w_gate"] if w == 1 else inp["ffn2_w_gate"]
                a = _colchunk(W[l], f, KD)
            elif kind == "u":
                W = inp["ffn1_w_up"] if w == 1 else inp["ffn2_w_up"]
                a = _colchunk(W[l], f, KD)
            else:
                W = inp["ffn1_w_down"] if w == 1 else inp["ffn2_w_down"]
                a = W[l][f * 128:(f + 1) * 128, :]
        elif tag[0] == "abin":
            a = _colchunk(inp["ab_w_in"][tag[1]], tag[2], KD)
        elif tag[0] == "poolw":
            pw = inp["pool_w"][tag[1]]
            PG = pw.shape[1]
            a = pw.reshape(4, cfg.GP, 128, PG).transpose(2, 0, 1, 3).reshape(128, -1)
        elif tag[0] == "about":
            a = inp["ab_w_out"][tag[1]][tag[2] * 128:(tag[2] + 1) * 128, :]
        elif tag[0] == "cin":
            _, i, kind, c = tag
            cc = c + (KD if kind in ("v", "v1") else 0)
            a = _colchunk(inp["c_w_in"][i], cc, KD)
        elif tag[0] == "cout":
            a = inp["c_w_out"][tag[1]][tag[2] * 128:(tag[2] + 1) * 128, :]
        else:
            raise ValueError(tag)
        out[n, :, :a.shape[1]] = a
    return out


class ParLayout:
    def __init__(s, cfg):
        s.off = {}
        n = 0

        def add(name, w):
            nonlocal n
            s.off[name] = (n, w)
            n += w

        for l in range(cfg.DEPTH):
            for w in range(3):
                add(("norm", l, w), cfg.KD)
        add(("fnorm",), cfg.KD)
        for i in range(cfg.N_EVEN):
            add(("convw", i), cfg.KC * CONV_W)
            add(("convb", i), cfg.KC)
            add(("clng", i), cfg.KC)
            add(("clnb", i), cfg.KC)
            add(("pscale", i), cfg.KP)
        for i in range(cfg.N_ODD):
            add(("slng", i), cfg.KD)
            add(("slnb", i), cfg.KD)
        add(("hmask",), HALO)
        add(("pcorr",), 4 * 16)
        add(("ident",), 128)
        s.n = n


def _fm(v):
    return np.ascontiguousarray(v.reshape(-1, 128).T)


def pack_params(cfg, inp, core):
    L = ParLayout(cfg)
    P = np.zeros((128, L.n), np.float32)

    def put(name, arr):
        o, w = L.off[name]
        assert arr.shape == (128, w), (name, arr.shape, w)
        P[:, o:o + w] = arr

    names = ["norm_ffn1", "norm_mix", "norm_ffn2"]
    for l in range(cfg.DEPTH):
        for w in range(3):
            put(("norm", l, w), _fm(inp[names[w]][l]))
    put(("fnorm",), _fm(inp["final_norm"]))
    for i in range(cfg.N_EVEN):
        cw = inp["conv_w"][i]
        a = cw.reshape(CONV_W, cfg.KC, 128).transpose(2, 1, 0).reshape(128, cfg.KC * CONV_W)
        put(("convw", i), a)
        put(("convb", i), _fm(inp["conv_b"][i]))
        put(("clng", i), _fm(inp["conv_ln_g"][i]))
        put(("clnb", i), _fm(inp["conv_ln_b"][i]))
        put(("pscale", i), _fm(inp["pool_scale"][i]))
    for i in range(cfg.N_ODD):
        put(("slng", i), _fm(inp["sgu_ln_g"][i]))
        put(("slnb", i), _fm(inp["sgu_ln_b"][i]))
    first = (core % cfg.CPS) == 0
    put(("hmask",), np.full((128, HALO), 0.0 if first else 1.0, np.float32))
    corr = np.ones((4, 16), np.float32)
    if first:
        for g, w in enumerate((2, 4, 8, 16)):
            for t in range(16):
                corr[g, t] = float(w) / float(min(t + 1, w))
    put(("pcorr",), np.broadcast_to(corr.reshape(1, 64), (128, 64)))
    put(("ident",), np.eye(128, dtype=np.float32))
    return P


def pack_oddp(cfg, inp):
    out = np.zeros((cfg.N_ODD, 128, 2048), np.float32)
    for i in range(cfg.N_ODD):
        w = inp["sgu_w"][i]
        out[i, :, :1024] = w.transpose(2, 0, 1).reshape(128, 1024)
        out[i, :, 1024:] = np.broadcast_to(inp["sgu_b"][i].reshape(1, 1024), (128, 1024))
    return out


def pack_x(cfg, x, core):
    b = core // cfg.CPS
    s0 = (core % cfg.CPS) * cfg.OWN
    T = HALO + cfg.OWN
    buf = np.zeros((T, cfg.D), np.float32)
    lo = s0 - HALO
    if lo >= 0:
        buf[:] = x[b, lo:lo + T]
    else:
        buf[-lo:] = x[b, 0:T + lo]
    return np.ascontiguousarray(buf.reshape(T, cfg.KD, 128).transpose(2, 1, 0))


class Tok:
    __slots__ = ("sem", "val", "eng")

    def __init__(s, sem, val, eng):
        s.sem, s.val, s.eng = sem, val, eng


class Eng:
    def __init__(s, name, sem, self_sync):
        s.name, s.sem, s.self_sync = name, sem, self_sync
        s.ops = []
        s.count = 0
        s.waited = {}
        s.last = None

    def wait(s, tok):
        if tok is None:
            return
        if tok.eng is s and not s.self_sync:
            return
        key = id(tok.sem)
        if s.waited.get(key, 0) >= tok.val:
            return
        s.waited[key] = tok.val
        sem, val = tok.sem, tok.val
        s.ops.append(lambda e: e.wait_ge(sem, val))

    def op(s, fn, signal=True):
        if signal:
            s.count += 1
            sem = s.sem
            s.ops.append(lambda e: fn(e).then_inc(sem, 1))
            s.last = Tok(s.sem, s.count, s)
            return s.last
        s.ops.append(fn)
        return None


class Res:
    __slots__ = ("w", "r")

    def __init__(s):
        s.w = None
        s.r = {}


def _deps(eng, reads, writes):
    for r in reads:
        eng.wait(r.w)
    for w in writes:
        eng.wait(w.w)
        for t in w.r.values():
            eng.wait(t)


def _mark(tok, eng, reads, writes, key=None):
    key = key or eng.name
    for r in reads:
        r.r[key] = tok
    for w in writes:
        w.w = tok
        w.r = {}


def emit(eng, fn, reads=(), writes=()):
    _deps(eng, reads, writes)
    tok = eng.op(fn, True)
    _mark(tok, eng, reads, writes)
    return tok


def op_act(out, in_, func, scale=None, bias=None, accum_out=None):
    kw = {}
    if scale is not None:
        kw["scale"] = scale
    if bias is not None:
        kw["bias"] = bias
    if accum_out is not None:
        kw["accum_out"] = accum_out
    return lambda e: e.activation(out=out, in_=in_, func=func, **kw)


def op_tt(out, in0, in1, op):
    return lambda e: e.tensor_tensor(out=out, in0=in0, in1=in1, op=op)


def op_ts(out, in0, s1, s2, op0, op1=None):
    if op1 is None:
        return lambda e: e.tensor_scalar(out=out, in0=in0, scalar1=s1, scalar2=None, op0=op0)
    return lambda e: e.tensor_scalar(out=out, in0=in0, scalar1=s1, scalar2=s2, op0=op0, op1=op1)


def op_stt(out, in0, scalar, in1, op0, op1):
    return lambda e: e.scalar_tensor_tensor(out=out, in0=in0, scalar=scalar, in1=in1, op0=op0, op1=op1)


def op_copy(out, in_):
    return lambda e: e.tensor_copy(out=out, in_=in_)


def op_memset(out, val):
    return lambda e: e.memset(out, val)


def op_recip(out, in_):
    return lambda e: e.reciprocal(out=out, in_=in_)


def op_mm(out, lhsT, rhs, start, stop):
    return lambda e: e.matmul(out, lhsT, rhs, start=start, stop=stop)


def op_dma(out, in_):
    return lambda e: e.dma_start(out=out, in_=in_)


class Buf:
    def __init__(s, ap):
        s.ap = ap
        s.res = Res()


class Rot:
    def __init__(s, bufs):
        s.bufs = bufs
        s.i = 0

    def next(s):
        b = s.bufs[s.i % len(s.bufs)]
        s.i += 1
        return b


class Arena:
    def __init__(s, ap_f32, nwords):
        s.ap, s.n, s.off = ap_f32, nwords, 0

    def reset(s):
        s.off = 0

    def alloc(s, cols, dtype):
        words = cols if dtype == F32 else (cols + 1) // 2
        words = (words + 7) // 8 * 8
        assert s.off + words <= s.n, ("arena overflow", s.off, words, s.n)
        v = s.ap[:, s.off:s.off + words]
        s.off += words
        if dtype != F32:
            v = v.bitcast(dtype)
        return v[:, 0:cols]


def build_program(cfg):
    nc = bass.Bass("TRN2", target_bir_lowering=False, dynamic_dma_scratch_size=256)
    KD, NF, KC, KP, GP = cfg.KD, cfg.NF, cfg.KC, cfg.KP, cfg.GP
    TA, TB, HALF = cfg.TA, cfg.TB, cfg.HALF
    TM = TA
    order = piece_order(cfg)
    NP = len(order)
    PL = ParLayout(cfg)

    d_x = nc.dram_tensor("xT", [128, KD, HALO + cfg.OWN], F32, kind="ExternalInput").ap()
    d_w = nc.dram_tensor("wst", [NP, 128, PIECE], F32, kind="ExternalInput").ap()
    d_par = nc.dram_tensor("par", [128, PL.n], F32, kind="ExternalInput").ap()
    d_odd = nc.dram_tensor("oddp", [cfg.N_ODD, 128, 2048], F32, kind="ExternalInput").ap()
    d_out = nc.dram_tensor("outT", [128, KD, cfg.OWN], F32, kind="ExternalOutput").ap()

    ARENA_WORDS = 17024
    from contextlib import ExitStack
    with ExitStack() as es:
        def sb(name, shape, dt):
            return es.enter_context(nc.sbuf_tensor(name, shape, dt))

        def sem(name):
            return es.enter_context(nc.semaphore(name))

        x_sb = sb("x_sb", [128, KD, TM], F32)
        h_sb = sb("h_sb", [128, KD, TM], BF16)
        ring = sb("ring", [128, cfg.NR, PIECE], BF16)
        stage = sb("stage", [128, cfg.NS, PIECE], F32)
        actb = sb("actb", [128, 2, cfg.G, TM], BF16)
        par = sb("par_sb", [128, PL.n], F32)
        ones32 = sb("ones32", [128, 128], F32)
        tmp4 = sb("tmp4", [128, 4, 512], F32)
        rstd_sb = sb("rstd", [128, TM], F32)
        ctxA = sb("ctxA", [128, cfg.N_EVEN, KC, 32], F32)
        ctxP = sb("ctxP", [128, cfg.N_EVEN, KP, 16], F32)
        arena_t = sb("arena", [128, ARENA_WORDS], F32)
        psum = [es.enter_context(nc.psum_tensor("ps%d" % i, [128, 512], F32)) for i in range(8)]

        PE = Eng("pe", sem("s_pe"), False)
        ACT = Eng("act", sem("s_act"), True)
        DVE = Eng("dve", sem("s_dve"), True)
        POOL = Eng("pool", sem("s_pool"), True)
        SP = Eng("sp", sem("s_sp"), False)
        stage_sem = [sem("s_st%d" % i) for i in range(cfg.NS)]
        stage_cnt = [0] * cfg.NS
        par_sem, x_sem, odd_sem = sem("s_par"), sem("s_x"), sem("s_odd")
        par_cnt, x_cnt, odd_cnt = [0], [0], [0]
        out_sem = sem("s_out")
        out_cnt = [0]

        def dma(eng, out, in_, semh, cnt, idx, reads=(), writes=()):
            _deps(eng, reads, writes)
            cnt[idx] += 16
            val = cnt[idx]
            eng.ops.append(lambda e: e.dma_start(out=out, in_=in_).then_inc(semh, 16))
            tok = Tok(semh, val, None)
            _mark(tok, eng, reads, writes, key=eng.name + "_dma")
            return tok

        bank = [Buf(psum[i][:]) for i in range(8)]
        rotA = Rot(bank[0:2])
        rotB = Rot(bank[2:4])
        rotD = Rot(bank[4:8])
        bankS0, bankS1 = bank[0], bank[2]
        tmpR = Rot([Buf(tmp4[:, i, :]) for i in range(4)])
        rtR = tmpR
        stage_res = [Res() for _ in range(cfg.NS)]
        ring_res = [Res() for _ in range(cfg.NR)]
        NSEG = 1 + (HALF + 511) // 512
        x_res = [[Res() for _ in range(KD)] for _ in range(NSEG)]
        h_res = [[Res() for _ in range(KD)] for _ in range(NSEG)]
        rstd_res = [Res() for _ in range(NSEG)]
        act_res = [[[Res() for _ in range(NSEG)] for _ in range(cfg.G)] for _ in range(2)]
        par_res = Res()
        ones_res = Res()
        ctxA_res = [[Res() for _ in range(KC)] for _ in range(cfg.N_EVEN)]
        ctxP_res = [[Res() for _ in range(KP)] for _ in range(cfg.N_EVEN)]
        arena = Arena(arena_t[:], ARENA_WORDS)

        def pcol(name, j=0, w=1):
            o, _ = PL.off[name]
            return par[:, o + j:o + j + w]

        def tiles(hf, lo):
            T = TA if hf == 0 else TB
            out = []
            if hf == 0:
                if lo < HALO:
                    out.append((0, lo, HALO - lo))
                c = HALO
            else:
                c = 0
            s = 1
            while c < T:
                n = min(512, T - c)
                out.append((s, c, n))
                c += n
                s += 1
            return out

        class WS:
            LOOK = 5

            def __init__(s):
                s.dma_emitted = 0
                s.cast_emitted = 0
                s.fetched = 0
                s.released = [False] * (2 * NP)

            def _dma(s, q):
                ss = q % cfg.NS
                dma(SP, stage[:, ss, :], d_w[q % NP], stage_sem[ss], stage_cnt, ss,
                    writes=[stage_res[ss]])
                s.dma_emitted += 1

            def advance(s):
                while s.cast_emitted < 2 * NP and s.cast_emitted < s.fetched + s.LOOK:
                    q = s.cast_emitted
                    if q >= cfg.NR and not s.released[q - cfg.NR]:
                        break
                    if s.dma_emitted <= q:
                        s._dma(q)
                    ss, rs = q % cfg.NS, q % cfg.NR
                    if False:
                        emit(POOL, op_copy(ring[:, rs, :], stage[:, ss, :]),
                             reads=[stage_res[ss]], writes=[ring_res[rs]])
                    else:
                        emit(ACT, op_act(ring[:, rs, :], stage[:, ss, :], AF.Copy),
                             reads=[stage_res[ss]], writes=[ring_res[rs]])
                    s.cast_emitted += 1
                    while s.dma_emitted < min(2 * NP, s.cast_emitted + cfg.NS):
                        s._dma(s.dma_emitted)

            def next(s, tag):
                q = s.fetched
                assert order[q % NP] == tag, (q, order[q % NP], tag)
                s.fetched += 1
                s.advance()
                assert s.cast_emitted > q, "weight ring too small (build-time deadlock) at %s" % (tag,)
                rs = q % cfg.NR
                return q, ring[:, rs, :], ring_res[rs]

            def release(s, q):
                s.released[q] = True
                s.advance()

        ws = WS()

        def mm_group(out_ap, pairs, reads, writes, step_reads=None):
            reads = list(reads) + list(step_reads or [])
            _deps(PE, reads, writes)
            n = len(pairs)
            tok = None
            for i, (l, r) in enumerate(pairs):
                tok = PE.op(op_mm(out_ap, l, r, i == 0, i == n - 1), signal=(i == n - 1))
            _mark(tok, PE, reads, writes)
            return tok

        def mm_step(ps, lhsT, rhs, first, last, reads):
            _deps(PE, reads, [ps.res] if first else [])
            tok = PE.op(op_mm(ps.ap[:, :rhs.shape[-1]], lhsT, rhs, first, last), True)
            for r in reads:
                r.r["pe"] = tok
            if first:
                ps.res.r = {}
            ps.res.w = tok
            return tok

        def barrier():
            toks = [PE.last, ACT.last, DVE.last]
            for e in (PE, ACT, DVE):
                for t in toks:
                    e.wait(t)

        def rmsnorm(hf, lo, gname, inplace=False, deferred=False):
            for (sg, c0, n) in tiles(hf, lo):
                ps = bankS0
                for k in range(KD):
                    sq = tmpR.next()
                    emit(ACT, op_act(sq.ap[:, :n], x_sb[:, k, c0:c0 + n], AF.Square),
                         reads=[x_res[sg][k]], writes=[sq.res])
                    if deferred:
                        emit(POOL, op_ts(h_sb[:, k, c0:c0 + n], x_sb[:, k, c0:c0 + n], pcol(gname, k), 1.0,
                                         ALU.mult, ALU.mult),
                             reads=[x_res[sg][k], par_res], writes=[h_res[sg][k]])
                    mm_step(ps, ones32[:], sq.ap[:, :n], k == 0, k == KD - 1, [sq.res, ones_res])
                rt = rtR.next()
                emit(ACT, op_act(rt.ap[:, :n], ps.ap[:, :n], AF.Sqrt, scale=1.0 / cfg.D, bias=EPS),
                     reads=[ps.res], writes=[rt.res])
                emit(DVE, op_recip(rstd_sb[:, c0:c0 + n], rt.ap[:, :n]),
                     reads=[rt.res], writes=[rstd_res[sg]])
                for k in range(KD):
                    if deferred:
                        break
                    if inplace:
                        emit(DVE, op_stt(x_sb[:, k, c0:c0 + n], x_sb[:, k, c0:c0 + n], pcol(gname, k),
                                         rstd_sb[:, c0:c0 + n], ALU.mult, ALU.mult),
                             reads=[rstd_res[sg], par_res], writes=[x_res[sg][k]])
                    else:
                        emit(DVE, op_stt(h_sb[:, k, c0:c0 + n], x_sb[:, k, c0:c0 + n], pcol(gname, k),
                                         rstd_sb[:, c0:c0 + n], ALU.mult, ALU.mult),
                             reads=[x_res[sg][k], rstd_res[sg], par_res], writes=[h_res[sg][k]])

        def proj_group(hf, lo, wl, src_fn, src_res_fn, scale):
            tl = tiles(hf, lo)
            for dm in range(KD):
                for (sg, c0, n) in tl:
                    pd = rotD.next()
                    pairs = [(w[1][:, dm * 128:(dm + 1) * 128], src_fn(i, c0, n)) for i, w in enumerate(wl)]
                    rd = [w[2] for w in wl] + [src_res_fn(i, sg) for i in range(len(wl))]
                    mm_group(pd.ap[:, :n], pairs, rd, [pd.res])
                    emit(DVE, op_stt(x_sb[:, dm, c0:c0 + n], pd.ap[:, :n], float(scale),
                                     x_sb[:, dm, c0:c0 + n], ALU.mult, ALU.add),
                         reads=[pd.res], writes=[x_res[sg][dm]])
            for w in wl:
                ws.release(w[0])

        def ffn(hf, l, w, lo):
            rmsnorm(hf, lo, ("norm", l, 0 if w == 1 else 2), deferred=True)
            groups = ffn_groups(cfg)
            tl = tiles(hf, lo)

            def GU(g):
                b = g % 2
                for fi, f in enumerate(groups[g]):
                    qg, wg, rg = ws.next(("ffn", l, w, "g", f))
                    qu, wu, ru = ws.next(("ffn", l, w, "u", f))
                    for (sg, c0, n) in tl:
                        pg, pu = rotA.next(), rotB.next()
                        mm_group(pg.ap[:, :n], [(wg[:, k * 128:(k + 1) * 128], h_sb[:, k, c0:c0 + n]) for k in range(KD)],
                                 [rg], [pg.res], step_reads=h_res[sg])
                        mm_group(pu.ap[:, :n], [(wu[:, k * 128:(k + 1) * 128], h_sb[:, k, c0:c0 + n]) for k in range(KD)],
                                 [ru], [pu.res], step_reads=h_res[sg])
                        sl, t2 = tmpR.next(), tmpR.next()
                        rs_ap = rstd_sb[:, c0:c0 + n]
                        emit(DVE, op_tt(sl.ap[:, :n], pg.ap[:, :n], rs_ap, ALU.mult),
                             reads=[pg.res, rstd_res[sg]], writes=[sl.res])
                        emit(ACT, op_act(sl.ap[:, :n], sl.ap[:, :n], AF.Silu), reads=[sl.res], writes=[sl.res])
                        emit(DVE, op_tt(t2.ap[:, :n], pu.ap[:, :n], rs_ap, ALU.mult),
                             reads=[pu.res, rstd_res[sg]], writes=[t2.res])
                        emit(DVE, op_tt(actb[:, b, fi, c0:c0 + n], sl.ap[:, :n], t2.ap[:, :n], ALU.mult),
                             reads=[sl.res, t2.res], writes=[act_res[b][fi][sg]])
                    ws.release(qg)
                    ws.release(qu)

            def DN(g):
                b = g % 2
                wl = [ws.next(("ffn", l, w, "d", f)) for f in groups[g]]
                proj_group(hf, lo, wl, lambda i, c0, n: actb[:, b, i, c0:c0 + n],
                           lambda i, sg: act_res[b][i][sg], 0.5)

            GU(0)
            for g in range(1, len(groups)):
                GU(g)
                DN(g - 1)
            DN(len(groups) - 1)

        def out_proj(hf, lo, tagfn, nK, src_ap_fn, src_res_fn, scale, GK=4, korder=None):
            korder = list(range(nK)) if korder is None else korder
            for k0 in range(0, nK, GK):
                ks = korder[k0:k0 + GK]
                wl = [ws.next(tagfn(kc)) for kc in ks]
                proj_group(hf, lo, wl, lambda i, c0, n: src_ap_fn(ks[i], c0, n),
                           lambda i, sg: src_res_fn(ks[i], sg), scale)

        def mixer_even(hf, l, lo_in, lo_out):
            i = l // 2
            T = TA if hf == 0 else TB
            rmsnorm(hf, lo_in, ("norm", l, 1), deferred=True)
            barrier()
            arena.reset()
            AG = [Buf(arena.alloc(32 + TM, BF16)) for _ in range(2)]
            DGS = [Buf(arena.alloc(128, BF16)) for _ in range(CONV_W)]
            Y = arena.alloc(KC * TM, F32).rearrange("p (k t) -> p k t", k=KC)
            Y_res = [[Res() for _ in range(NSEG)] for _ in range(KC)]
            PB = [Buf(arena.alloc(16 + TM, F32)) for _ in range(2)]
            PT = [Buf(arena.alloc(16 + TM, F32)) for _ in range(2)]
            PLD = arena.alloc(GP * TM, BF16).rearrange("p (k t) -> p k t", k=GP)
            PLD_res = [Res() for _ in range(GP)]
            CATB = arena.alloc(KP * TM, BF16).rearrange("p (k t) -> p k t", k=KP)
            CATB_res = [[Res() for _ in range(NSEG)] for _ in range(KP)]

            def cat_ap(kc, c0, n):
                return h_sb[:, kc, c0:c0 + n] if kc < KC else CATB[:, kc - KC, c0:c0 + n]

            def cat_res(kc, sg):
                return h_res[sg][kc] if kc < KC else CATB_res[kc - KC][sg]
            MEAN = arena.alloc(TM, F32)
            RSTD = arena.alloc(TM, F32)
            M2 = arena.alloc(TM, F32)
            st_res = [Res() for _ in range(NSEG)]
            tl_in = tiles(hf, lo_in)
            tl_out = tiles(hf, lo_out)
            nout = T - lo_out
            def gen_diags(j):
                o, _ = PL.off[("convw", i)]
                for k in range(CONV_W):
                    wk = par[:, o + j * CONV_W + k:o + j * CONV_W + k + 1]
                    emit(DVE, op_ts(DGS[k].ap, pcol(("ident",), 0, 128), wk, None, ALU.mult),
                         reads=[par_res], writes=[DGS[k].res])

            def conv(j):
                ag = AG[j % 2]
                pcs = [rotD.next() for _ in tl_out]
                tok = None
                for k in range(CONV_W):
                    dg = DGS[k]
                    _deps(PE, [dg.res, ag.res], [pc.res for pc in pcs] if k == 0 else [])
                    for ti, (sg, c0, n) in enumerate(tl_out):
                        last = (ti == len(tl_out) - 1)
                        tok = PE.op(op_mm(pcs[ti].ap[:, :n], dg.ap, ag.ap[:, 2 + c0 + k:2 + c0 + k + n],
                                          k == 0, k == CONV_W - 1), signal=last)
                    dg.res.r["pe"] = tok
                ag.res.r["pe"] = tok
                for ti, (sg, c0, n) in enumerate(tl_out):
                    pcs[ti].res.w = tok
                    pcs[ti].res.r = {}
                    emit(DVE, op_ts(Y[:, j, c0:c0 + n], pcs[ti].ap[:, :n], pcol(("convb", i), j), None, ALU.add),
                         reads=[pcs[ti].res, par_res], writes=[Y_res[j][sg]])

            for j in range(KC):
                if j > 0:
                    gen_diags(j - 1)
                qa, wa, ra = ws.next(("abin", i, j))
                qg, wg, rg = ws.next(("abin", i, KC + j))
                ag = AG[j % 2]
                if hf == 0:
                    emit(DVE, op_memset(ag.ap[:, 0:32], 0.0), writes=[ag.res])
                else:
                    emit(DVE, op_copy(ag.ap[:, 0:32], ctxA[:, i, j, :]), reads=[ctxA_res[i][j]], writes=[ag.res])
                for (sg, c0, n) in tl_in:
                    pa, pg = rotA.next(), rotB.next()
                    mm_group(pa.ap[:, :n], [(wa[:, k * 128:(k + 1) * 128], h_sb[:, k, c0:c0 + n]) for k in range(KD)],
                             [ra], [pa.res], step_reads=h_res[sg])
                    mm_group(pg.ap[:, :n], [(wg[:, k * 128:(k + 1) * 128], h_sb[:, k, c0:c0 + n]) for k in range(KD)],
                             [rg], [pg.res], step_reads=h_res[sg])
                    sg_t, a_t = tmpR.next(), tmpR.next()
                    rs_ap = rstd_sb[:, c0:c0 + n]
                    emit(DVE, op_tt(sg_t.ap[:, :n], pg.ap[:, :n], rs_ap, ALU.mult),
                         reads=[pg.res, rstd_res[sg]], writes=[sg_t.res])
                    emit(ACT, op_act(sg_t.ap[:, :n], sg_t.ap[:, :n], AF.Sigmoid), reads=[sg_t.res], writes=[sg_t.res])
                    emit(DVE, op_tt(a_t.ap[:, :n], pa.ap[:, :n], rs_ap, ALU.mult),
                         reads=[pa.res, rstd_res[sg]], writes=[a_t.res])
                    emit(DVE, op_tt(ag.ap[:, 32 + c0:32 + c0 + n], sg_t.ap[:, :n], a_t.ap[:, :n], ALU.mult),
                         reads=[sg_t.res, a_t.res], writes=[ag.res])
                    if hf == 0 and c0 < HALO:
                        emit(DVE, op_tt(ag.ap[:, 32 + c0:32 + HALO], ag.ap[:, 32 + c0:32 + HALO],
                                        pcol(("hmask",), c0, HALO - c0), ALU.mult),
                             reads=[par_res], writes=[ag.res])
                ws.release(qa)
                ws.release(qg)
                if hf == 0:
                    emit(ACT, op_act(ctxA[:, i, j, :], ag.ap[:, 32 + T - 32:32 + T], AF.Copy),
                         reads=[ag.res], writes=[ctxA_res[i][j]])
                if j > 0:
                    conv(j - 1)
            gen_diags(KC - 1)
            conv(KC - 1)
            wpool = None
            for grp in range(4):
                win = 2 << grp
                for jj in range(GP):
                    j = grp * GP + jj
                    qp, wp, rp = ws.next(("abin", i, 2 * KC + j))
                    pb = PB[j % 2]
                    if hf == 0:
                        emit(DVE, op_memset(pb.ap[:, 0:16], 0.0), writes=[pb.res])
                    else:
                        emit(DVE, op_copy(pb.ap[:, 0:16], ctxP[:, i, j, :]), reads=[ctxP_res[i][j]], writes=[pb.res])
                    for (sg, c0, n) in tl_in:
                        pp = rotA.next()
                        mm_group(pp.ap[:, :n], [(wp[:, k * 128:(k + 1) * 128], h_sb[:, k, c0:c0 + n]) for k in range(KD)],
                                 [rp] + h_res[sg], [pp.res])
                        emit(DVE, op_tt(pb.ap[:, 16 + c0:16 + c0 + n], pp.ap[:, :n], rstd_sb[:, c0:c0 + n], ALU.mult),
                             reads=[pp.res, rstd_res[sg]], writes=[pb.res])
                        if hf == 0 and c0 < HALO:
                            emit(DVE, op_tt(pb.ap[:, 16 + c0:16 + HALO], pb.ap[:, 16 + c0:16 + HALO],
                                            pcol(("hmask",), c0, HALO - c0), ALU.mult),
                                 reads=[par_res], writes=[pb.res])
                    ws.release(qp)
                    if hf == 0:
                        emit(ACT, op_act(ctxP[:, i, j, :], pb.ap[:, 16 + T - 16:16 + T], AF.Copy),
                             reads=[pb.res], writes=[ctxP_res[i][j]])
                    cur = pb
                    lo_b = (16 + lo_in) if hf == 0 else 0
                    s = 1
                    lvl = 0
                    while s < win:
                        nxt = PT[lvl % 2]
                        a0 = lo_b + s
                        emit(DVE, op_tt(nxt.ap[:, a0:16 + T], cur.ap[:, a0:16 + T], cur.ap[:, a0 - s:16 + T - s], ALU.add),
                             reads=[cur.res], writes=[nxt.res])
                        cur = nxt
                        lo_b = a0
                        s *= 2
                        lvl += 1
                    if hf == 0:
                        o, _ = PL.off[("pcorr",)]
                        emit(DVE, op_tt(cur.ap[:, 16 + HALO:16 + HALO + 16], cur.ap[:, 16 + HALO:16 + HALO + 16],
                                        par[:, o + grp * 16:o + grp * 16 + 16], ALU.mult),
                             reads=[par_res], writes=[cur.res])
                    emit(DVE, op_stt(PLD[:, jj, lo_out:T], cur.ap[:, 16 + lo_out:16 + T], 1.0 / win,
                                     pb.ap[:, 16 + lo_out:16 + T], ALU.mult, ALU.subtract),
                         reads=[cur.res, pb.res], writes=[PLD_res[jj]])
                if grp == 0:
                    wpool = ws.next(("poolw", i))
                PG = GP * 128
                for dj in range(GP):
                    for (sg, c0, n) in tl_out:
                        pq = rotB.next()
                        pairs = []
                        for kc in range(GP):
                            base = (grp * GP + kc) * PG + dj * 128
                            pairs.append((wpool[1][:, base:base + 128], PLD[:, kc, c0:c0 + n]))
                        mm_group(pq.ap[:, :n], pairs, [wpool[2]] + PLD_res, [pq.res])
                        emit(DVE, op_ts(CATB[:, grp * GP + dj, c0:c0 + n], pq.ap[:, :n],
                                        pcol(("pscale", i), grp * GP + dj), None, ALU.mult),
                             reads=[pq.res, par_res], writes=[CATB_res[grp * GP + dj][sg]])
            ws.release(wpool[0])
            for (sg, c0, n) in tl_out:
                p1, p2 = bankS0, bankS1
                for j in range(KC):
                    mm_step(p1, ones32[:], Y[:, j, c0:c0 + n], j == 0, j == KC - 1, [Y_res[j][sg], ones_res])
                for j in range(KC):
                    sq = tmpR.next()
                    emit(ACT, op_act(sq.ap[:, :n], Y[:, j, c0:c0 + n], AF.Square), reads=[Y_res[j][sg]], writes=[sq.res])
                    mm_step(p2, ones32[:], sq.ap[:, :n], j == 0, j == KC - 1, [sq.res, ones_res])
                DC = float(KC * 128)
                emit(DVE, op_ts(MEAN[:, c0:c0 + n], p1.ap[:, :n], 1.0 / DC, None, ALU.mult),
                     reads=[p1.res], writes=[st_res[sg]])
                emit(DVE, op_tt(M2[:, c0:c0 + n], MEAN[:, c0:c0 + n], MEAN[:, c0:c0 + n], ALU.mult),
                     reads=[st_res[sg]], writes=[st_res[sg]])
                emit(DVE, op_stt(M2[:, c0:c0 + n], p2.ap[:, :n], 1.0 / DC, M2[:, c0:c0 + n], ALU.mult, ALU.subtract),
                     reads=[p2.res], writes=[st_res[sg]])
                rt = rtR.next()
                emit(ACT, op_act(rt.ap[:, :n], M2[:, c0:c0 + n], AF.Sqrt, bias=EPS), reads=[st_res[sg]], writes=[rt.res])
                emit(DVE, op_recip(RSTD[:, c0:c0 + n], rt.ap[:, :n]), reads=[rt.res], writes=[st_res[sg]])
            out_proj(hf, lo_out, lambda kc: ("about", i, kc), KC,
                     cat_ap, cat_res, 1.0, korder=list(range(KC, 2 * KC)))
            for (sg, c0, n) in tl_out:
                for j in range(KC):
                    emit(DVE, op_tt(Y[:, j, c0:c0 + n], Y[:, j, c0:c0 + n], MEAN[:, c0:c0 + n], ALU.subtract),
                         reads=[st_res[sg]], writes=[Y_res[j][sg]])
                    emit(DVE, op_tt(Y[:, j, c0:c0 + n], Y[:, j, c0:c0 + n], RSTD[:, c0:c0 + n], ALU.mult),
                         reads=[st_res[sg]], writes=[Y_res[j][sg]])
                    emit(ACT, op_act(h_sb[:, j, c0:c0 + n], Y[:, j, c0:c0 + n], AF.Silu,
                                     scale=pcol(("clng", i), j), bias=pcol(("clnb", i), j)),
                         reads=[Y_res[j][sg], par_res], writes=[h_res[sg][j]])
            out_proj(hf, lo_out, lambda kc: ("about", i, kc), KC,
                     cat_ap, cat_res, 1.0, korder=list(range(KC)))
            barrier()

        def mixer_odd(hf, l, lo_in, lo_out):
            i = l // 2
            T = TA if hf == 0 else TB
            rmsnorm(hf, lo_in, ("norm", l, 1))
            barrier()
            arena.reset()
            nW = (T - lo_in) // 128
            assert nW * 128 == T - lo_in
            D = cfg.D
            TW_ = T - lo_in
            ODDP = Buf(arena.alloc(2048, F32))
            WT = ODDP.ap[:, 0:1024].rearrange("p (g i) -> p g i", g=8)
            SGB = ODDP.ap[:, 1024:2048].rearrange("p (g i) -> p g i", g=8)
            WTb = Buf(arena.alloc(1024, BF16))
            WTb3 = WTb.ap.rearrange("p (g i) -> p g i", g=8)
            RS = Buf(arena.alloc(1024, F32))
            RS3 = RS.ap.rearrange("p (g i) -> p g i", g=8)
            VN = arena.alloc(nW * D, BF16).rearrange("p (w d) -> p w d", w=nW)
            VN_res = [Res() for _ in range(nW)]
            UB = [Buf(arena.alloc(TW_, F32)) for _ in range(2)]
            TWB = [Buf(arena.alloc(128, F32)) for _ in range(2)]
            B2 = [Buf(arena.alloc(128, F32)) for _ in range(2)]
            JK = Buf(arena.alloc(128, F32))
            S1 = Buf(arena.alloc(nW * KD, F32))
            S2 = Buf(arena.alloc(nW * KD, F32))
            S13 = S1.ap.rearrange("p (w c) -> p w c", w=nW)
            S23 = S2.ap.rearrange("p (w c) -> p w c", w=nW)
            MR = Buf(arena.alloc(8 * nW, F32))
            s1r, s2r = MR.ap[:, 0:nW], MR.ap[:, nW:2 * nW]
            mean, m2 = MR.ap[:, 2 * nW:3 * nW], MR.ap[:, 3 * nW:4 * nW]
            var, rsd = MR.ap[:, 4 * nW:5 * nW], MR.ap[:, 5 * nW:6 * nW]
            GT = Rot([Buf(arena.alloc(512, F32)) for _ in range(2)])
            Gb = arena.alloc(KD * TW_, BF16).rearrange("p (k t) -> p k t", k=KD)
            G_res = [[Res() for _ in range(NSEG)] for _ in range(KD)]

            dma(ACT, ODDP.ap, d_odd[i], odd_sem, odd_cnt, 0, writes=[ODDP.res])
            emit(DVE, op_memset(WT[64:128, :, 0:64], 0.0), writes=[ODDP.res])
            emit(DVE, op_copy(WTb.ap, ODDP.ap[:, 0:1024]), reads=[ODDP.res], writes=[WTb.res])
            for hh in range(2):
                pr = rotD.next()
                _deps(PE, [ODDP.res, ones_res], [pr.res])
                tok = PE.op(op_mm(pr.ap[:, :512], ones32[:], ODDP.ap[:, hh * 512:(hh + 1) * 512], True, True), True)
                _mark(tok, PE, [ODDP.res], [pr.res])
                emit(ACT, op_act(RS.ap[:, hh * 512:(hh + 1) * 512], pr.ap[:, :512], AF.Copy), reads=[pr.res], writes=[RS.res])

            wins = list(range(nW))
            allh = [r for sg in range(NSEG) for r in h_res[sg]]

            def vmm(c, wv, rv, wb):
                pv = rotD.next()
                _deps(PE, [rv] + allh, [pv.res])
                tok = None
                for wi, w in enumerate(wb):
                    t0 = lo_in + w * 128
                    for k in range(KD):
                        tok = PE.op(op_mm(pv.ap[:, wi * 128:(wi + 1) * 128], h_sb[:, k, t0:t0 + 128],
                                          wv[:, k * 128:(k + 1) * 128], k == 0, k == KD - 1),
                                    signal=(wi == len(wb) - 1 and k == KD - 1))
                _mark(tok, PE, [rv] + allh, [pv.res])
                gt = GT.next()
                nb = len(wb)
                emit(ACT, op_act(gt.ap[:, :nb * 128], pv.ap[:, :nb * 128], AF.Gelu_apprx_tanh),
                     reads=[pv.res], writes=[gt.res])
                return gt

            for c in range(KD):
                qv, wv, rv = ws.next(("cin", i, "v1", c))
                for w0 in range(0, nW, 4):
                    wb = wins[w0:w0 + 4]
                    gt = vmm(c, wv, rv, wb)
                    for wi, w in enumerate(wb):
                        gsl = gt.ap[:, wi * 128:(wi + 1) * 128]
                        s1c, s2c = S13[:, w, c:c + 1], S23[:, w, c:c + 1]
                        emit(DVE, lambda e, gsl=gsl, s1c=s1c: e.tensor_scalar(
                                 out=JK.ap, in0=gsl, scalar1=1.0, scalar2=None, op0=ALU.mult, op1=ALU.add, accum_out=s1c),
                             reads=[gt.res], writes=[JK.res, S1.res])
                        emit(DVE, lambda e, gsl=gsl, s2c=s2c: e.scalar_tensor_tensor(
                                 out=JK.ap, in0=gsl, scalar=1.0, in1=gsl, op0=ALU.mult, op1=ALU.mult, accum_out=s2c),
                             reads=[gt.res], writes=[JK.res, S2.res])
                ws.release(qv)
            for w in range(nW):
                emit(ACT, op_act(JK.ap[:, 0:KD], S13[:, w, :], AF.Copy, accum_out=s1r[:, w:w + 1]),
                     reads=[S1.res], writes=[JK.res, MR.res])
                emit(ACT, op_act(JK.ap[:, 0:KD], S23[:, w, :], AF.Copy, accum_out=s2r[:, w:w + 1]),
                     reads=[S2.res], writes=[JK.res, MR.res])
            emit(DVE, op_ts(mean, s1r, 1.0 / D, None, ALU.mult), reads=[MR.res], writes=[MR.res])
            emit(DVE, op_tt(m2, mean, mean, ALU.mult), reads=[MR.res], writes=[MR.res])
            emit(DVE, op_stt(var, s2r, 1.0 / D, m2, ALU.mult, ALU.subtract), reads=[MR.res], writes=[MR.res])
            emit(ACT, op_act(var, var, AF.Sqrt, bias=EPS), reads=[MR.res], writes=[MR.res])
            emit(DVE, op_recip(rsd, var), reads=[MR.res], writes=[MR.res])
            for c in range(KD):
                qv, wv, rv = ws.next(("cin", i, "v", c))
                for w0 in range(0, nW, 4):
                    wb = wins[w0:w0 + 4]
                    gt = vmm(c, wv, rv, wb)
                    for wi, w in enumerate(wb):
                        emit(DVE, op_ts(VN[:, w, c * 128:(c + 1) * 128], gt.ap[:, wi * 128:(wi + 1) * 128],
                                        mean[:, w:w + 1], rsd[:, w:w + 1], ALU.subtract, ALU.mult),
                             reads=[gt.res, MR.res], writes=[VN_res[w]])
                ws.release(qv)
            tl_in = tiles(hf, lo_in)
            for c in range(KD):
                g = c // cfg.HC
                qu, wu, ru = ws.next(("cin", i, "u", c))
                ub = UB[c % 2]
                b2 = B2[c % 2]
                emit(DVE, op_stt(b2.ap, RS3[:, g, :], pcol(("slnb", i), c), SGB[:, g, :], ALU.mult, ALU.add),
                     reads=[RS.res, ODDP.res, par_res], writes=[b2.res])
                for (sg, c0, n) in tl_in:
                    pu = rotB.next()
                    mm_group(pu.ap[:, :n], [(wu[:, k * 128:(k + 1) * 128], h_sb[:, k, c0:c0 + n]) for k in range(KD)],
                             [ru] + h_res[sg], [pu.res])
                    emit(ACT, op_act(ub.ap[:, c0 - lo_in:c0 - lo_in + n], pu.ap[:, :n], AF.Gelu_apprx_tanh),
                         reads=[pu.res], writes=[ub.res])
                ws.release(qu)
                for w0 in range(0, nW, 4):
                    wb = wins[w0:w0 + 4]
                    pm = rotA.next()
                    rd = [VN_res[w] for w in wb] + [WTb.res]
                    _deps(PE, rd, [pm.res])
                    tok = None
                    for wi, w in enumerate(wb):
                        tok = PE.op(op_mm(pm.ap[:, wi * 128:(wi + 1) * 128], VN[:, w, c * 128:(c + 1) * 128],
                                          WTb3[:, g, :], True, True), signal=(wi == len(wb) - 1))
                    _mark(tok, PE, rd, [pm.res])
                    for wi, w in enumerate(wb):
                        t0 = lo_in + w * 128
                        tw = TWB[w % 2]
                        emit(DVE, op_stt(tw.ap, pm.ap[:, wi * 128:(wi + 1) * 128], pcol(("slng", i), c),
                                         b2.ap, ALU.mult, ALU.add),
                             reads=[pm.res, b2.res, par_res], writes=[tw.res])
                        sgs = [sg for (sg, c0, n) in tl_in if c0 < t0 + 128 and t0 < c0 + n]
                        emit(DVE, op_tt(Gb[:, c, t0 - lo_in:t0 - lo_in + 128], tw.ap, ub.ap[:, t0 - lo_in:t0 - lo_in + 128], ALU.mult),
                             reads=[tw.res, ub.res], writes=[G_res[c][sg] for sg in sgs])
            out_proj(hf, lo_out, lambda kc: ("cout", i, kc), KD,
                     lambda kc, c0, n: Gb[:, kc, c0 - lo_in:c0 - lo_in + n], lambda kc, sg: G_res[kc][sg], 1.0)
            barrier()

        emit(DVE, op_memset(ones32[:], 1.0), writes=[ones_res])
        dma(SP, par[:], d_par, par_sem, par_cnt, 0, writes=[par_res])
        LO_IN = [0, 32, 128, 160]
        LO_OUT = [32, 128, 160, 160]
        for hf in range(2):
            T = TA if hf == 0 else TB
            off = 0 if hf == 0 else TA
            allx = [r for sg in range(NSEG) for r in x_res[sg]]
            dma(SP, x_sb[:, :, 0:T], d_x[:, :, off:off + T], x_sem, x_cnt, 0, writes=allx)
            for l in range(cfg.DEPTH):
                li = LO_IN[l] if hf == 0 else 0
                lo = LO_OUT[l] if hf == 0 else 0
                ffn(hf, l, 1, li)
                if l % 2 == 0:
                    mixer_even(hf, l, li, lo)
                else:
                    mixer_odd(hf, l, li, lo)
                ffn(hf, l, 2, lo)
            lo = HALO if hf == 0 else 0
            rmsnorm(hf, lo, ("fnorm",), inplace=True)
            ooff = 0 if hf == 0 else HALF
            dma(ACT, d_out[:, :, ooff:ooff + HALF], x_sb[:, :, lo:lo + HALF], out_sem, out_cnt, 0, reads=allx)
        assert ws.fetched == 2 * NP, (ws.fetched, NP)
        fin = Tok(out_sem, out_cnt[0], None)
        ACT.wait(fin)

        engs = {"tensor": PE, "scalar": ACT, "vector": DVE, "gpsimd": POOL, "sync": SP}
        with nc.Block() as block:
            for name, E in engs.items():
                def body(e, E=E):
                    for f in E.ops:
                        f(e)
                getattr(block, name)(body)
        stats = {k: len(v.ops) for k, v in engs.items()}
    return nc, stats


_CACHE = {}


def run(cfg, inputs, trace=False):
    inp = {k: np.asarray(v) for k, v in inputs.items()}
    wst = pack_weights(cfg, inp)
    oddp = pack_oddp(cfg, inp)
    in_maps = []
    for c in range(cfg.NCORES):
        in_maps.append({"xT": pack_x(cfg, inp["x"], c), "wst": wst,
                        "par": pack_params(cfg, inp, c), "oddp": oddp})
    nc, stats = build_program(cfg)
    res = run_bass_kernel_spmd(nc, in_maps, core_ids=list(range(cfg.NCORES)), trace=trace)
    out = np.zeros((cfg.BATCH, cfg.SEQ, cfg.D), np.float32)
    for c in range(cfg.NCORES):
        o = res.results[c]["outT"]
        b = c // cfg.CPS
        s0 = (c % cfg.CPS) * cfg.OWN
        out[b, s0:s0 + cfg.OWN, :] = o.transpose(2, 1, 0).reshape(cfg.OWN, cfg.D)
    return out, res, stats


def kernel(**inputs):
    cfg = Cfg()
    out, _, _ = run(cfg, inputs)
    return out
```

```python
import numpy as np
import concourse.bass as bass
import concourse.mybir as mybir
from concourse.bass_utils import run_bass_kernel_spmd

F32 = mybir.dt.float32
BF16 = mybir.dt.bfloat16
AF = mybir.ActivationFunctionType
ALU = mybir.AluOpType

EPS = 1e-6
CONV_W = 31
PIECE = 2048
HALO = 160


class Cfg:
    def __init__(s, D=2048, DFF=5504, SEQ=4096, BATCH=2, DEPTH=4, NCORES=8,
                 G=4, NR=10, NS=3):
        s.D, s.DFF, s.SEQ, s.BATCH, s.DEPTH, s.NCORES = D, DFF, SEQ, BATCH, DEPTH, NCORES
        s.KD = D // 128
        s.NF = DFF // 128
        s.CPS = NCORES // BATCH
        s.OWN = SEQ // s.CPS
        s.HALF = s.OWN // 2
        s.TA = HALO + s.HALF
        s.TB = s.HALF
        s.KC = (D // 2) // 128
        s.KP = (D // 2) // 128
        s.GP = s.KP // 4
        s.HC = s.KD // 8
        s.G = G
        s.NR, s.NS = NR, NS
        s.N_EVEN = (DEPTH + 1) // 2
        s.N_ODD = DEPTH // 2
        assert s.KD * 128 <= PIECE and D <= PIECE and s.HALF % 128 == 0
        assert DEPTH == 4


def ffn_groups(cfg):
    sizes = [cfg.G] * (cfg.NF // cfg.G)
    rem = cfg.NF % cfg.G
    if rem:
        sizes.append(rem)
    if len(sizes) >= 2 and sizes[-1] == 1 and cfg.G >= 3:
        sizes[-2:] = [cfg.G - 1, 2]
    out, i = [], 0
    for n in sizes:
        out.append(list(range(i, i + n)))
        i += n
    assert i == cfg.NF
    return out


def ffn_order(cfg, l, w):
    groups = ffn_groups(cfg)
    seq = []

    def GU(g):
        for f in groups[g]:
            seq.append(("ffn", l, w, "g", f))
            seq.append(("ffn", l, w, "u", f))

    def DN(g):
        for f in groups[g]:
            seq.append(("ffn", l, w, "d", f))

    GU(0)
    for g in range(1, len(groups)):
        GU(g)
        DN(g - 1)
    DN(len(groups) - 1)
    return seq


def about_korder(cfg):
    return list(range(cfg.KC, 2 * cfg.KC)) + list(range(cfg.KC))


def mixer_order(cfg, l):
    i = l // 2
    seq = []
    if l % 2 == 0:
        for j in range(cfg.KC):
            seq.append(("abin", i, j))
            seq.append(("abin", i, cfg.KC + j))
        for grp in range(4):
            for jj in range(cfg.GP):
                seq.append(("abin", i, 2 * cfg.KC + grp * cfg.GP + jj))
            if grp == 0:
                seq.append(("poolw", i))
        for kc in about_korder(cfg):
            seq.append(("about", i, kc))
    else:
        for c in range(cfg.KD):
            seq.append(("cin", i, "v1", c))
        for c in range(cfg.KD):
            seq.append(("cin", i, "v", c))
        for c in range(cfg.KD):
            seq.append(("cin", i, "u", c))
        for kc in range(cfg.KD):
            seq.append(("cout", i, kc))
    return seq


def piece_order(cfg):
    seq = []
    for l in range(cfg.DEPTH):
        seq += ffn_order(cfg, l, 1)
        seq += mixer_order(cfg, l)
        seq += ffn_order(cfg, l, 2)
    return seq


def _colchunk(W, c, KD):
    blk = W[:, c * 128:(c + 1) * 128]
    return blk.reshape(KD, 128, 128).transpose(1, 0, 2).reshape(128, KD * 128)


def pack_weights(cfg, inp):
    order = piece_order(cfg)
    out = np.zeros((len(order), 128, PIECE), np.float32)
    KD = cfg.KD
    for n, tag in enumerate(order):
        if tag[0] == "ffn":
            _, l, w, kind, f = tag
            if kind == "g":
                W = inp["ffn1_w_gate"] if w == 1 else inp["ffn2_w_gate"]
                a = _colchunk(W[l], f, KD)
            elif kind == "u":
                W = inp["ffn1_w_up"] if w == 1 else inp["ffn2_w_up"]
                a = _colchunk(W[l], f, KD)
            else:
                W = inp["ffn1_w_down"] if w == 1 else inp["ffn2_w_down"]
                a = W[l][f * 128:(f + 1) * 128, :]
        elif tag[0] == "abin":
            a = _colchunk(inp["ab_w_in"][tag[1]], tag[2], KD)
        elif tag[0] == "poolw":
            pw = inp["pool_w"][tag[1]]
            PG = pw.shape[1]
            a = pw.reshape(4, cfg.GP, 128, PG).transpose(2, 0, 1, 3).reshape(128, -1)
        elif tag[0] == "about":
            a = inp["ab_w_out"][tag[1]][tag[2] * 128:(tag[2] + 1) * 128, :]
        elif tag[0] == "cin":
            _, i, kind, c = tag
            cc = c + (KD if kind in ("v", "v1") else 0)
            a = _colchunk(inp["c_w_in"][i], cc, KD)
        elif tag[0] == "cout":
            a = inp["c_w_out"][tag[1]][tag[2] * 128:(tag[2] + 1) * 128, :]
        else:
            raise ValueError(tag)
        out[n, :, :a.shape[1]] = a
    return out


class ParLayout:
    def __init__(s, cfg):
        s.off = {}
        n = 0

        def add(name, w):
            nonlocal n
            s.off[name] = (n, w)
            n += w

        for l in range(cfg.DEPTH):
            for w in range(3):
                add(("norm", l, w), cfg.KD)
        add(("fnorm",), cfg.KD)
        for i in range(cfg.N_EVEN):
            add(("convw", i), cfg.KC * CONV_W)
            add(("convb", i), cfg.KC)
            add(("clng", i), cfg.KC)
            add(("clnb", i), cfg.KC)
            add(("pscale", i), cfg.KP)
        for i in range(cfg.N_ODD):
            add(("slng", i), cfg.KD)
            add(("slnb", i), cfg.KD)
        add(("hmask",), HALO)
        add(("pcorr",), 4 * 16)
        add(("ident",), 128)
        s.n = n


def _fm(v):
    return np.ascontiguousarray(v.reshape(-1, 128).T)


def pack_params(cfg, inp, core):
    L = ParLayout(cfg)
    P = np.zeros((128, L.n), np.float32)

    def put(name, arr):
        o, w = L.off[name]
        assert arr.shape == (128, w), (name, arr.shape, w)
        P[:, o:o + w] = arr

    names = ["norm_ffn1", "norm_mix", "norm_ffn2"]
    for l in range(cfg.DEPTH):
        for w in range(3):
            put(("norm", l, w), _fm(inp[names[w]][l]))
    put(("fnorm",), _fm(inp["final_norm"]))
    for i in range(cfg.N_EVEN):
        cw = inp["conv_w"][i]
        a = cw.reshape(CONV_W, cfg.KC, 128).transpose(2, 1, 0).reshape(128, cfg.KC * CONV_W)
        put(("convw", i), a)
        put(("convb", i), _fm(inp["conv_b"][i]))
        put(("clng", i), _fm(inp["conv_ln_g"][i]))
        put(("clnb", i), _fm(inp["conv_ln_b"][i]))
        put(("pscale", i), _fm(inp["pool_scale"][i]))
    for i in range(cfg.N_ODD):
        put(("slng", i), _fm(inp["sgu_ln_g"][i]))
        put(("slnb", i), _fm(inp["sgu_ln_b"][i]))
    first = (core % cfg.CPS) == 0
    put(("hmask",), np.full((128, HALO), 0.0 if first else 1.0, np.float32))
    corr = np.ones((4, 16), np.float32)
    if first:
        for g, w in enumerate((2, 4, 8, 16)):
            for t in range(16):
                corr[g, t] = float(w) / float(min(t + 1, w))
    put(("pcorr",), np.broadcast_to(corr.reshape(1, 64), (128, 64)))
    put(("ident",), np.eye(128, dtype=np.float32))
    return P


def pack_oddp(cfg, inp):
    out = np.zeros((cfg.N_ODD, 128, 2048), np.float32)
    for i in range(cfg.N_ODD):
        w = inp["sgu_w"][i]
        out[i, :, :1024] = w.transpose(2, 0, 1).reshape(128, 1024)
        out[i, :, 1024:] = np.broadcast_to(inp["sgu_b"][i].reshape(1, 1024), (128, 1024))
    return out


def pack_x(cfg, x, core):
    b = core // cfg.CPS
    s0 = (core % cfg.CPS) * cfg.OWN
    T = HALO + cfg.OWN
    buf = np.zeros((T, cfg.D), np.float32)
    lo = s0 - HALO
    if lo >= 0:
        buf[:] = x[b, lo:lo + T]
    else:
        buf[-lo:] = x[b, 0:T + lo]
    return np.ascontiguousarray(buf.reshape(T, cfg.KD, 128).transpose(2, 1, 0))


class Tok:
    __slots__ = ("sem", "val", "eng")

    def __init__(s, sem, val, eng):
        s.sem, s.val, s.eng = sem, val, eng


class Eng:
    def __init__(s, name, sem, self_sync):
        s.name, s.sem, s.self_sync = name, sem, self_sync
        s.ops = []
        s.count = 0
        s.waited = {}
        s.last = None

    def wait(s, tok):
        if tok is None:
            return
        if tok.eng is s and not s.self_sync:
            return
        key = id(tok.sem)
        if s.waited.get(key, 0) >= tok.val:
            return
        s.waited[key] = tok.val
        sem, val = tok.sem, tok.val
        s.ops.append(lambda e: e.wait_ge(sem, val))

    def op(s, fn, signal=True):
        if signal:
            s.count += 1
            sem = s.sem
            s.ops.append(lambda e: fn(e).then_inc(sem, 1))
            s.last = Tok(s.sem, s.count, s)
            return s.last
        s.ops.append(fn)
        return None


class Res:
    __slots__ = ("w", "r")

    def __init__(s):
        s.w = None
        s.r = {}


def _deps(eng, reads, writes):
    for r in reads:
        eng.wait(r.w)
    for w in writes:
        eng.wait(w.w)
        for t in w.r.values():
            eng.wait(t)


def _mark(tok, eng, reads, writes, key=None):
    key = key or eng.name
    for r in reads:
        r.r[key] = tok
    for w in writes:
        w.w = tok
        w.r = {}


def emit(eng, fn, reads=(), writes=()):
    _deps(eng, reads, writes)
    tok = eng.op(fn, True)
    _mark(tok, eng, reads, writes)
    return tok


def op_act(out, in_, func, scale=None, bias=None, accum_out=None):
    kw = {}
    if scale is not None:
        kw["scale"] = scale
    if bias is not None:
        kw["bias"] = bias
    if accum_out is not None:
        kw["accum_out"] = accum_out
    return lambda e: e.activation(out=out, in_=in_, func=func, **kw)


def op_tt(out, in0, in1, op):
    return lambda e: e.tensor_tensor(out=out, in0=in0, in1=in1, op=op)


def op_ts(out, in0, s1, s2, op0, op1=None):
    if op1 is None:
        return lambda e: e.tensor_scalar(out=out, in0=in0, scalar1=s1, scalar2=None, op0=op0)
    return lambda e: e.tensor_scalar(out=out, in0=in0, scalar1=s1, scalar2=s2, op0=op0, op1=op1)


def op_stt(out, in0, scalar, in1, op0, op1):
    return lambda e: e.scalar_tensor_tensor(out=out, in0=in0, scalar=scalar, in1=in1, op0=op0, op1=op1)


def op_copy(out, in_):
    return lambda e: e.tensor_copy(out=out, in_=in_)


def op_memset(out, val):
    return lambda e: e.memset(out, val)


def op_recip(out, in_):
    return lambda e: e.reciprocal(out=out, in_=in_)


def op_mm(out, lhsT, rhs, start, stop):
    return lambda e: e.matmul(out, lhsT, rhs, start=start, stop=stop)


def op_dma(out, in_):
    return lambda e: e.dma_start(out=out, in_=in_)


class Buf:
    def __init__(s, ap):
        s.ap = ap
        s.res = Res()


class Rot:
    def __init__(s, bufs):
        s.bufs = bufs
        s.i = 0

    def next(s):
        b = s.bufs[s.i % len(s.bufs)]
        s.i += 1
        return b


class Arena:
    def __init__(s, ap_f32, nwords):
        s.ap, s.n, s.off = ap_f32, nwords, 0

    def reset(s):
        s.off = 0

    def alloc(s, cols, dtype):
        words = cols if dtype == F32 else (cols + 1) // 2
        words = (words + 7) // 8 * 8
        assert s.off + words <= s.n, ("arena overflow", s.off, words, s.n)
        v = s.ap[:, s.off:s.off + words]
        s.off += words
        if dtype != F32:
            v = v.bitcast(dtype)
        return v[:, 0:cols]


def build_program(cfg):
    nc = bass.Bass("TRN2", target_bir_lowering=False, dynamic_dma_scratch_size=256)
    KD, NF, KC, KP, GP = cfg.KD, cfg.NF, cfg.KC, cfg.KP, cfg.GP
    TA, TB, HALF = cfg.TA, cfg.TB, cfg.HALF
    TM = TA
    order = piece_order(cfg)
    NP = len(order)
    PL = ParLayout(cfg)

    d_x = nc.dram_tensor("xT", [128, KD, HALO + cfg.OWN], F32, kind="ExternalInput").ap()
    d_w = nc.dram_tensor("wst", [NP, 128, PIECE], F32, kind="ExternalInput").ap()
    d_par = nc.dram_tensor("par", [128, PL.n], F32, kind="ExternalInput").ap()
    d_odd = nc.dram_tensor("oddp", [cfg.N_ODD, 128, 2048], F32, kind="ExternalInput").ap()
    d_out = nc.dram_tensor("outT", [128, KD, cfg.OWN], F32, kind="ExternalOutput").ap()

    ARENA_WORDS = 17024
    from contextlib import ExitStack
    with ExitStack() as es:
        def sb(name, shape, dt):
            return es.enter_context(nc.sbuf_tensor(name, shape, dt))

        def sem(name):
            return es.enter_context(nc.semaphore(name))

        x_sb = sb("x_sb", [128, KD, TM], F32)
        h_sb = sb("h_sb", [128, KD, TM], BF16)
        ring = sb("ring", [128, cfg.NR, PIECE], BF16)
        stage = sb("stage", [128, cfg.NS, PIECE], F32)
        actb = sb("actb", [128, 2, cfg.G, TM], BF16)
        par = sb("par_sb", [128, PL.n], F32)
        ones32 = sb("ones32", [128, 128], F32)
        tmp4 = sb("tmp4", [128, 4, 512], F32)
        rstd_sb = sb("rstd", [128, TM], F32)
        ctxA = sb("ctxA", [128, cfg.N_EVEN, KC, 32], F32)
        ctxP = sb("ctxP", [128, cfg.N_EVEN, KP, 16], F32)
        arena_t = sb("arena", [128, ARENA_WORDS], F32)
        psum = [es.enter_context(nc.psum_tensor("ps%d" % i, [128, 512], F32)) for i in range(8)]

        PE = Eng("pe", sem("s_pe"), False)
        ACT = Eng("act", sem("s_act"), True)
        DVE = Eng("dve", sem("s_dve"), True)
        POOL = Eng("pool", sem("s_pool"), True)
        SP = Eng("sp", sem("s_sp"), False)
        stage_sem = [sem("s_st%d" % i) for i in range(cfg.NS)]
        stage_cnt = [0] * cfg.NS
        par_sem, x_sem, odd_sem = sem("s_par"), sem("s_x"), sem("s_odd")
        par_cnt, x_cnt, odd_cnt = [0], [0], [0]
        out_sem = sem("s_out")
        out_cnt = [0]

        def dma(eng, out, in_, semh, cnt, idx, reads=(), writes=()):
            _deps(eng, reads, writes)
            cnt[idx] += 16
            val = cnt[idx]
            eng.ops.append(lambda e: e.dma_start(out=out, in_=in_).then_inc(semh, 16))
            tok = Tok(semh, val, None)
            _mark(tok, eng, reads, writes, key=eng.name + "_dma")
            return tok

        bank = [Buf(psum[i][:]) for i in range(8)]
        rotA = Rot(bank[0:2])
        rotB = Rot(bank[2:4])
        rotD = Rot(bank[4:8])
        bankS0, bankS1 = bank[0], bank[2]
        tmpR = Rot([Buf(tmp4[:, i, :]) for i in range(4)])
        rtR = tmpR
        stage_res = [Res() for _ in range(cfg.NS)]
        ring_res = [Res() for _ in range(cfg.NR)]
        NSEG = 1 + (HALF + 511) // 512
        x_res = [[Res() for _ in range(KD)] for _ in range(NSEG)]
        h_res = [[Res() for _ in range(KD)] for _ in range(NSEG)]
        rstd_res = [Res() for _ in range(NSEG)]
        act_res = [[[Res() for _ in range(NSEG)] for _ in range(cfg.G)] for _ in range(2)]
        par_res = Res()
        ones_res = Res()
        ctxA_res = [[Res() for _ in range(KC)] for _ in range(cfg.N_EVEN)]
        ctxP_res = [[Res() for _ in range(KP)] for _ in range(cfg.N_EVEN)]
        arena = Arena(arena_t[:], ARENA_WORDS)

        def pcol(name, j=0, w=1):
            o, _ = PL.off[name]
            return par[:, o + j:o + j + w]

        def tiles(hf, lo):
            T = TA if hf == 0 else TB
            out = []
            if hf == 0:
                if lo < HALO:
                    out.append((0, lo, HALO - lo))
                c = HALO
            else:
                c = 0
            s = 1
            while c < T:
                n = min(512, T - c)
                out.append((s, c, n))
                c += n
                s += 1
            return out

        class WS:
            LOOK = 5

            def __init__(s):
                s.dma_emitted = 0
                s.cast_emitted = 0
                s.fetched = 0
                s.released = [False] * (2 * NP)

            def _dma(s, q):
                ss = q % cfg.NS
                dma(SP, stage[:, ss, :], d_w[q % NP], stage_sem[ss], stage_cnt, ss,
                    writes=[stage_res[ss]])
                s.dma_emitted += 1

            def advance(s):
                while s.cast_emitted < 2 * NP and s.cast_emitted < s.fetched + s.LOOK:
                    q = s.cast_emitted
                    if q >= cfg.NR and not s.released[q - cfg.NR]:
                        break
                    if s.dma_emitted <= q:
                        s._dma(q)
                    ss, rs = q % cfg.NS, q % cfg.NR
                    if False:
                        emit(POOL, op_copy(ring[:, rs, :], stage[:, ss, :]),
                             reads=[stage_res[ss]], writes=[ring_res[rs]])
                    else:
                        emit(ACT, op_act(ring[:, rs, :], stage[:, ss, :], AF.Copy),
                             reads=[stage_res[ss]], writes=[ring_res[rs]])
                    s.cast_emitted += 1
                    while s.dma_emitted < min(2 * NP, s.cast_emitted + cfg.NS):
                        s._dma(s.dma_emitted)

            def next(s, tag):
                q = s.fetched
                assert order[q % NP] == tag, (q, order[q % NP], tag)
                s.fetched += 1
                s.advance()
                assert s.cast_emitted > q, "weight ring too small (build-time deadlock) at %s" % (tag,)
                rs = q % cfg.NR
                return q, ring[:, rs, :], ring_res[rs]

            def release(s, q):
                s.released[q] = True
                s.advance()

        ws = WS()

        def mm_group(out_ap, pairs, reads, writes, step_reads=None):
            reads = list(reads) + list(step_reads or [])
            _deps(PE, reads, writes)
            n = len(pairs)
            tok = None
            for i, (l, r) in enumerate(pairs):
                tok = PE.op(op_mm(out_ap, l, r, i == 0, i == n - 1), signal=(i == n - 1))
            _mark(tok, PE, reads, writes)
            return tok

        def mm_step(ps, lhsT, rhs, first, last, reads):
            _deps(PE, reads, [ps.res] if first else [])
            tok = PE.op(op_mm(ps.ap[:, :rhs.shape[-1]], lhsT, rhs, first, last), True)
            for r in reads:
                r.r["pe"] = tok
            if first:
                ps.res.r = {}
            ps.res.w = tok
            return tok

        def barrier():
            toks = [PE.last, ACT.last, DVE.last]
            for e in (PE, ACT, DVE):
                for t in toks:
                    e.wait(t)

        def rmsnorm(hf, lo, gname, inplace=False, deferred=False):
            for (sg, c0, n) in tiles(hf, lo):
                ps = bankS0
                for k in range(KD):
                    sq = tmpR.next()
                    emit(ACT, op_act(sq.ap[:, :n], x_sb[:, k, c0:c0 + n], AF.Square),
                         reads=[x_res[sg][k]], writes=[sq.res])
                    if deferred:
                        emit(POOL, op_ts(h_sb[:, k, c0:c0 + n], x_sb[:, k, c0:c0 + n], pcol(gname, k), 1.0,
                                         ALU.mult, ALU.mult),
                             reads=[x_res[sg][k], par_res], writes=[h_res[sg][k]])
                    mm_step(ps, ones32[:], sq.ap[:, :n], k == 0, k == KD - 1, [sq.res, ones_res])
                rt = rtR.next()
                emit(ACT, op_act(rt.ap[:, :n], ps.ap[:, :n], AF.Sqrt, scale=1.0 / cfg.D, bias=EPS),
                     reads=[ps.res], writes=[rt.res])
                emit(DVE, op_recip(rstd_sb[:, c0:c0 + n], rt.ap[:, :n]),
                     reads=[rt.res], writes=[rstd_res[sg]])
                for k in range(KD):
                    if deferred:
                        break
                    if inplace:
                        emit(DVE, op_stt(x_sb[:, k, c0:c0 + n], x_sb[:, k, c0:c0 + n], pcol(gname, k),
                                         rstd_sb[:, c0:c0 + n], ALU.mult, ALU.mult),
                             reads=[rstd_res[sg], par_res], writes=[x_res[sg][k]])
                    else:
                        emit(DVE, op_stt(h_sb[:, k, c0:c0 + n], x_sb[:, k, c0:c0 + n], pcol(gname, k),
                                         rstd_sb[:, c0:c0 + n], ALU.mult, ALU.mult),
                             reads=[x_res[sg][k], rstd_res[sg], par_res], writes=[h_res[sg][k]])

        def proj_group(hf, lo, wl, src_fn, src_res_fn, scale):
            tl = tiles(hf, lo)
            for dm in range(KD):
                for (sg, c0, n) in tl:
                    pd = rotD.next()
                    pairs = [(w[1][:, dm * 128:(dm + 1) * 128], src_fn(i, c0, n)) for i, w in enumerate(wl)]
                    rd = [w[2] for w in wl] + [src_res_fn(i, sg) for i in range(len(wl))]
                    mm_group(pd.ap[:, :n], pairs, rd, [pd.res])
                    emit(DVE, op_stt(x_sb[:, dm, c0:c0 + n], pd.ap[:, :n], float(scale),
                                     x_sb[:, dm, c0:c0 + n], ALU.mult, ALU.add),
                         reads=[pd.res], writes=[x_res[sg][dm]])
            for w in wl:
                ws.release(w[0])

        def ffn(hf, l, w, lo):
            rmsnorm(hf, lo, ("norm", l, 0 if w == 1 else 2), deferred=True)
            groups = ffn_groups(cfg)
            tl = tiles(hf, lo)

            def GU(g):
                b = g % 2
                for fi, f in enumerate(groups[g]):
                    qg, wg, rg = ws.next(("ffn", l, w, "g", f))
                    qu, wu, ru = ws.next(("ffn", l, w, "u", f))
                    for (sg, c0, n) in tl:
                        pg, pu = rotA.next(), rotB.next()
                        mm_group(pg.ap[:, :n], [(wg[:, k * 128:(k + 1) * 128], h_sb[:, k, c0:c0 + n]) for k in range(KD)],
                                 [rg], [pg.res], step_reads=h_res[sg])
                        mm_group(pu.ap[:, :n], [(wu[:, k * 128:(k + 1) * 128], h_sb[:, k, c0:c0 + n]) for k in range(KD)],
                                 [ru], [pu.res], step_reads=h_res[sg])
                        sl, t2 = tmpR.next(), tmpR.next()
                        rs_ap = rstd_sb[:, c0:c0 + n]
                        emit(DVE, op_tt(sl.ap[:, :n], pg.ap[:, :n], rs_ap, ALU.mult),
                             reads=[pg.res, rstd_res[sg]], writes=[sl.res])
                        emit(ACT, op_act(sl.ap[:, :n], sl.ap[:, :n], AF.Silu), reads=[sl.res], writes=[sl.res])
                        emit(DVE, op_tt(t2.ap[:, :n], pu.ap[:, :n], rs_ap, ALU.mult),
                             reads=[pu.res, rstd_res[sg]], writes=[t2.res])
                        emit(DVE, op_tt(actb[:, b, fi, c0:c0 + n], sl.ap[:, :n], t2.ap[:, :n], ALU.mult),
                             reads=[sl.res, t2.res], writes=[act_res[b][fi][sg]])
                    ws.release(qg)
                    ws.release(qu)

            def DN(g):
                b = g % 2
                wl = [ws.next(("ffn", l, w, "d", f)) for f in groups[g]]
                proj_group(hf, lo, wl, lambda i, c0, n: actb[:, b, i, c0:c0 + n],
                           lambda i, sg: act_res[b][i][sg], 0.5)

            GU(0)
            for g in range(1, len(groups)):
                GU(g)
                DN(g - 1)
            DN(len(groups) - 1)

        def out_proj(hf, lo, tagfn, nK, src_ap_fn, src_res_fn, scale, GK=4, korder=None):
            korder = list(range(nK)) if korder is None else korder
            for k0 in range(0, nK, GK):
                ks = korder[k0:k0 + GK]
                wl = [ws.next(tagfn(kc)) for kc in ks]
                proj_group(hf, lo, wl, lambda i, c0, n: src_ap_fn(ks[i], c0, n),
                           lambda i, sg: src_res_fn(ks[i], sg), scale)

        def mixer_even(hf, l, lo_in, lo_out):
            i = l // 2
            T = TA if hf == 0 else TB
            rmsnorm(hf, lo_in, ("norm", l, 1), deferred=True)
            barrier()
            arena.reset()
            AG = [Buf(arena.alloc(32 + TM, BF16)) for _ in range(2)]
            DGS = [Buf(arena.alloc(128, BF16)) for _ in range(CONV_W)]
            Y = arena.alloc(KC * TM, F32).rearrange("p (k t) -> p k t", k=KC)
            Y_res = [[Res() for _ in range(NSEG)] for _ in range(KC)]
            PB = [Buf(arena.alloc(16 + TM, F32)) for _ in range(2)]
            PT = [Buf(arena.alloc(16 + TM, F32)) for _ in range(2)]
            PLD2 = [arena.alloc(GP * TM, BF16).rearrange("p (k t) -> p k t", k=GP) for _ in range(2)]
            PLD2_res = [[Res() for _ in range(GP)] for _ in range(2)]
            CATB = arena.alloc(KP * TM, BF16).rearrange("p (k t) -> p k t", k=KP)
            CATB_res = [[Res() for _ in range(NSEG)] for _ in range(KP)]

            def cat_ap(kc, c0, n):
                return h_sb[:, kc, c0:c0 + n] if kc < KC else CATB[:, kc - KC, c0:c0 + n]

            def cat_res(kc, sg):
                return h_res[sg][kc] if kc < KC else CATB_res[kc - KC][sg]
            MEAN = arena.alloc(TM, F32)
            RSTD = arena.alloc(TM, F32)
            M2 = arena.alloc(TM, F32)
            st_res = [Res() for _ in range(NSEG)]
            tl_in = tiles(hf, lo_in)
            tl_out = tiles(hf, lo_out)
            nout = T - lo_out
            def gen_diags(j):
                o, _ = PL.off[("convw", i)]
                for k in range(CONV_W):
                    wk = par[:, o + j * CONV_W + k:o + j * CONV_W + k + 1]
                    emit(DVE, op_ts(DGS[k].ap, pcol(("ident",), 0, 128), wk, None, ALU.mult),
                         reads=[par_res], writes=[DGS[k].res])

            def conv(j):
                ag = AG[j % 2]
                pcs = [rotD.next() for _ in tl_out]
                tok = None
                for k in range(CONV_W):
                    dg = DGS[k]
                    _deps(PE, [dg.res, ag.res], [pc.res for pc in pcs] if k == 0 else [])
                    for ti, (sg, c0, n) in enumerate(tl_out):
                        last = (ti == len(tl_out) - 1)
                        tok = PE.op(op_mm(pcs[ti].ap[:, :n], dg.ap, ag.ap[:, 2 + c0 + k:2 + c0 + k + n],
                                          k == 0, k == CONV_W - 1), signal=last)
                    dg.res.r["pe"] = tok
                ag.res.r["pe"] = tok
                for ti, (sg, c0, n) in enumerate(tl_out):
                    pcs[ti].res.w = tok
                    pcs[ti].res.r = {}
                    emit(DVE, op_ts(Y[:, j, c0:c0 + n], pcs[ti].ap[:, :n], pcol(("convb", i), j), None, ALU.add),
                         reads=[pcs[ti].res, par_res], writes=[Y_res[j][sg]])

            for j in range(KC):
                if j > 0:
                    gen_diags(j - 1)
                qa, wa, ra = ws.next(("abin", i, j))
                qg, wg, rg = ws.next(("abin", i, KC + j))
                ag = AG[j % 2]
                if hf == 0:
                    emit(DVE, op_memset(ag.ap[:, 0:32], 0.0), writes=[ag.res])
                else:
                    emit(DVE, op_copy(ag.ap[:, 0:32], ctxA[:, i, j, :]), reads=[ctxA_res[i][j]], writes=[ag.res])
                for (sg, c0, n) in tl_in:
                    pa, pg = rotA.next(), rotB.next()
                    mm_group(pa.ap[:, :n], [(wa[:, k * 128:(k + 1) * 128], h_sb[:, k, c0:c0 + n]) for k in range(KD)],
                             [ra], [pa.res], step_reads=h_res[sg])
                    mm_group(pg.ap[:, :n], [(wg[:, k * 128:(k + 1) * 128], h_sb[:, k, c0:c0 + n]) for k in range(KD)],
                             [rg], [pg.res], step_reads=h_res[sg])
                    sg_t, a_t = tmpR.next(), tmpR.next()
                    rs_ap = rstd_sb[:, c0:c0 + n]
                    emit(DVE, op_tt(sg_t.ap[:, :n], pg.ap[:, :n], rs_ap, ALU.mult),
                         reads=[pg.res, rstd_res[sg]], writes=[sg_t.res])
                    emit(ACT, op_act(sg_t.ap[:, :n], sg_t.ap[:, :n], AF.Sigmoid), reads=[sg_t.res], writes=[sg_t.res])
                    emit(DVE, op_tt(a_t.ap[:, :n], pa.ap[:, :n], rs_ap, ALU.mult),
                         reads=[pa.res, rstd_res[sg]], writes=[a_t.res])
                    emit(DVE, op_tt(ag.ap[:, 32 + c0:32 + c0 + n], sg_t.ap[:, :n], a_t.ap[:, :n], ALU.mult),
                         reads=[sg_t.res, a_t.res], writes=[ag.res])
                    if hf == 0 and c0 < HALO:
                        emit(DVE, op_tt(ag.ap[:, 32 + c0:32 + HALO], ag.ap[:, 32 + c0:32 + HALO],
                                        pcol(("hmask",), c0, HALO - c0), ALU.mult),
                             reads=[par_res], writes=[ag.res])
                ws.release(qa)
                ws.release(qg)
                if hf == 0:
                    emit(ACT, op_act(ctxA[:, i, j, :], ag.ap[:, 32 + T - 32:32 + T], AF.Copy),
                         reads=[ag.res], writes=[ctxA_res[i][j]])
                if j > 0:
                    conv(j - 1)
            gen_diags(KC - 1)
            conv(KC - 1)
            def pool_mm(grp, wpool):
                PG = GP * 128
                PLD, PLD_res = PLD2[grp % 2], PLD2_res[grp % 2]
                for dj in range(GP):
                    for (sg, c0, n) in tl_out:
                        pq = rotD.next()
                        pairs = []
                        for kc in range(GP):
                            base = (grp * GP + kc) * PG + dj * 128
                            pairs.append((wpool[1][:, base:base + 128], PLD[:, kc, c0:c0 + n]))
                        mm_group(pq.ap[:, :n], pairs, [wpool[2]] + PLD_res, [pq.res])
                        emit(DVE, op_ts(CATB[:, grp * GP + dj, c0:c0 + n], pq.ap[:, :n],
                                        pcol(("pscale", i), grp * GP + dj), None, ALU.mult),
                             reads=[pq.res, par_res], writes=[CATB_res[grp * GP + dj][sg]])

            wpool = None
            for grp in range(4):
                win = 2 << grp
                for jj in range(GP):
                    j = grp * GP + jj
                    qp, wp, rp = ws.next(("abin", i, 2 * KC + j))
                    pb = PB[j % 2]
                    if hf == 0:
                        emit(DVE, op_memset(pb.ap[:, 0:16], 0.0), writes=[pb.res])
                    else:
                        emit(DVE, op_copy(pb.ap[:, 0:16], ctxP[:, i, j, :]), reads=[ctxP_res[i][j]], writes=[pb.res])
                    for (sg, c0, n) in tl_in:
                        pp = rotA.next()
                        mm_group(pp.ap[:, :n], [(wp[:, k * 128:(k + 1) * 128], h_sb[:, k, c0:c0 + n]) for k in range(KD)],
                                 [rp] + h_res[sg], [pp.res])
                        emit(DVE, op_tt(pb.ap[:, 16 + c0:16 + c0 + n], pp.ap[:, :n], rstd_sb[:, c0:c0 + n], ALU.mult),
                             reads=[pp.res, rstd_res[sg]], writes=[pb.res])
                        if hf == 0 and c0 < HALO:
                            emit(DVE, op_tt(pb.ap[:, 16 + c0:16 + HALO], pb.ap[:, 16 + c0:16 + HALO],
                                            pcol(("hmask",), c0, HALO - c0), ALU.mult),
                                 reads=[par_res], writes=[pb.res])
                    ws.release(qp)
                    if hf == 0:
                        emit(ACT, op_act(ctxP[:, i, j, :], pb.ap[:, 16 + T - 16:16 + T], AF.Copy),
                             reads=[pb.res], writes=[ctxP_res[i][j]])
                    cur = pb
                    lo_b = (16 + lo_in) if hf == 0 else 0
                    s = 1
                    lvl = 0
                    while s < win:
                        nxt = PT[lvl % 2]
                        a0 = lo_b + s
                        emit(DVE, op_tt(nxt.ap[:, a0:16 + T], cur.ap[:, a0:16 + T], cur.ap[:, a0 - s:16 + T - s], ALU.add),
                             reads=[cur.res], writes=[nxt.res])
                        cur = nxt
                        lo_b = a0
                        s *= 2
                        lvl += 1
                    if hf == 0:
                        o, _ = PL.off[("pcorr",)]
                        emit(DVE, op_tt(cur.ap[:, 16 + HALO:16 + HALO + 16], cur.ap[:, 16 + HALO:16 + HALO + 16],
                                        par[:, o + grp * 16:o + grp * 16 + 16], ALU.mult),
                             reads=[par_res], writes=[cur.res])
                    emit(DVE, op_stt(PLD2[grp % 2][:, jj, lo_out:T], cur.ap[:, 16 + lo_out:16 + T], 1.0 / win,
                                     pb.ap[:, 16 + lo_out:16 + T], ALU.mult, ALU.subtract),
                         reads=[cur.res, pb.res], writes=[PLD2_res[grp % 2][jj]])
                if grp == 0:
                    wpool = ws.next(("poolw", i))
                if grp > 0:
                    pool_mm(grp - 1, wpool)
            pool_mm(3, wpool)
            ws.release(wpool[0])
            for (sg, c0, n) in tl_out:
                p1, p2 = bankS0, bankS1
                for j in range(KC):
                    mm_step(p1, ones32[:], Y[:, j, c0:c0 + n], j == 0, j == KC - 1, [Y_res[j][sg], ones_res])
                for j in range(KC):
                    sq = tmpR.next()
                    emit(ACT, op_act(sq.ap[:, :n], Y[:, j, c0:c0 + n], AF.Square), reads=[Y_res[j][sg]], writes=[sq.res])
                    mm_step(p2, ones32[:], sq.ap[:, :n], j == 0, j == KC - 1, [sq.res, ones_res])
                DC = float(KC * 128)
                emit(DVE, op_ts(MEAN[:, c0:c0 + n], p1.ap[:, :n], 1.0 / DC, None, ALU.mult),
                     reads=[p1.res], writes=[st_res[sg]])
                emit(DVE, op_tt(M2[:, c0:c0 + n], MEAN[:, c0:c0 + n], MEAN[:, c0:c0 + n], ALU.mult),
                     reads=[st_res[sg]], writes=[st_res[sg]])
                emit(DVE, op_stt(M2[:, c0:c0 + n], p2.ap[:, :n], 1.0 / DC, M2[:, c0:c0 + n], ALU.mult, ALU.subtract),
                     reads=[p2.res], writes=[st_res[sg]])
                rt = rtR.next()
                emit(ACT, op_act(rt.ap[:, :n], M2[:, c0:c0 + n], AF.Sqrt, bias=EPS), reads=[st_res[sg]], writes=[rt.res])
                emit(DVE, op_recip(RSTD[:, c0:c0 + n], rt.ap[:, :n]), reads=[rt.res], writes=[st_res[sg]])
            out_proj(hf, lo_out, lambda kc: ("about", i, kc), KC,
                     cat_ap, cat_res, 1.0, korder=list(range(KC, 2 * KC)))
            for (sg, c0, n) in tl_out:
                for j in range(KC):
                    emit(DVE, op_tt(Y[:, j, c0:c0 + n], Y[:, j, c0:c0 + n], MEAN[:, c0:c0 + n], ALU.subtract),
                         reads=[st_res[sg]], writes=[Y_res[j][sg]])
                    emit(DVE, op_tt(Y[:, j, c0:c0 + n], Y[:, j, c0:c0 + n], RSTD[:, c0:c0 + n], ALU.mult),
                         reads=[st_res[sg]], writes=[Y_res[j][sg]])
                    emit(ACT, op_act(h_sb[:, j, c0:c0 + n], Y[:, j, c0:c0 + n], AF.Silu,
                                     scale=pcol(("clng", i), j), bias=pcol(("clnb", i), j)),
                         reads=[Y_res[j][sg], par_res], writes=[h_res[sg][j]])
            out_proj(hf, lo_out, lambda kc: ("about", i, kc), KC,
                     cat_ap, cat_res, 1.0, korder=list(range(KC)))
            barrier()

        def mixer_odd(hf, l, lo_in, lo_out):
            i = l // 2
            T = TA if hf == 0 else TB
            rmsnorm(hf, lo_in, ("norm", l, 1))
            barrier()
            arena.reset()
            nW = (T - lo_in) // 128
            assert nW * 128 == T - lo_in
            D = cfg.D
            TW_ = T - lo_in
            ODDP = Buf(arena.alloc(2048, F32))
            WT = ODDP.ap[:, 0:1024].rearrange("p (g i) -> p g i", g=8)
            SGB = ODDP.ap[:, 1024:2048].rearrange("p (g i) -> p g i", g=8)
            WTb = Buf(arena.alloc(1024, BF16))
            WTb3 = WTb.ap.rearrange("p (g i) -> p g i", g=8)
            RS = Buf(arena.alloc(1024, F32))
            RS3 = RS.ap.rearrange("p (g i) -> p g i", g=8)
            VN = arena.alloc(nW * D, BF16).rearrange("p (w d) -> p w d", w=nW)
            VN_res = [Res() for _ in range(nW)]
            UB = [Buf(arena.alloc(TW_, F32)) for _ in range(2)]
            TWB = [Buf(arena.alloc(128, F32)) for _ in range(2)]
            B2 = [Buf(arena.alloc(128, F32)) for _ in range(2)]
            JK = Buf(arena.alloc(128, F32))
            S1 = Buf(arena.alloc(nW * KD, F32))
            S2 = Buf(arena.alloc(nW * KD, F32))
            S13 = S1.ap.rearrange("p (w c) -> p w c", w=nW)
            S23 = S2.ap.rearrange("p (w c) -> p w c", w=nW)
            MR = Buf(arena.alloc(8 * nW, F32))
            s1r, s2r = MR.ap[:, 0:nW], MR.ap[:, nW:2 * nW]
            mean, m2 = MR.ap[:, 2 * nW:3 * nW], MR.ap[:, 3 * nW:4 * nW]
            var, rsd = MR.ap[:, 4 * nW:5 * nW], MR.ap[:, 5 * nW:6 * nW]
            GT = Rot([Buf(arena.alloc(512, F32)) for _ in range(2)])
            Gb = arena.alloc(KD * TW_, BF16).rearrange("p (k t) -> p k t", k=KD)
            G_res = [[Res() for _ in range(NSEG)] for _ in range(KD)]

            dma(ACT, ODDP.ap, d_odd[i], odd_sem, odd_cnt, 0, writes=[ODDP.res])
            emit(DVE, op_memset(WT[64:128, :, 0:64], 0.0), writes=[ODDP.res])
            emit(DVE, op_copy(WTb.ap, ODDP.ap[:, 0:1024]), reads=[ODDP.res], writes=[WTb.res])
            for hh in range(2):
                pr = rotD.next()
                _deps(PE, [ODDP.res, ones_res], [pr.res])
                tok = PE.op(op_mm(pr.ap[:, :512], ones32[:], ODDP.ap[:, hh * 512:(hh + 1) * 512], True, True), True)
                _mark(tok, PE, [ODDP.res], [pr.res])
                emit(ACT, op_act(RS.ap[:, hh * 512:(hh + 1) * 512], pr.ap[:, :512], AF.Copy), reads=[pr.res], writes=[RS.res])

            wins = list(range(nW))
            allh = [r for sg in range(NSEG) for r in h_res[sg]]

            def vmm(c, wv, rv, wb):
                pv = rotD.next()
                _deps(PE, [rv] + allh, [pv.res])
                tok = None
                for wi, w in enumerate(wb):
                    t0 = lo_in + w * 128
                    for k in range(KD):
                        tok = PE.op(op_mm(pv.ap[:, wi * 128:(wi + 1) * 128], h_sb[:, k, t0:t0 + 128],
                                          wv[:, k * 128:(k + 1) * 128], k == 0, k == KD - 1),
                                    signal=(wi == len(wb) - 1 and k == KD - 1))
                _mark(tok, PE, [rv] + allh, [pv.res])
                gt = GT.next()
                nb = len(wb)
                emit(ACT, op_act(gt.ap[:, :nb * 128], pv.ap[:, :nb * 128], AF.Gelu_apprx_tanh),
                     reads=[pv.res], writes=[gt.res])
                return gt

            for c in range(KD):
                qv, wv, rv = ws.next(("cin", i, "v1", c))
                for w0 in range(0, nW, 4):
                    wb = wins[w0:w0 + 4]
                    gt = vmm(c, wv, rv, wb)
                    for wi, w in enumerate(wb):
                        gsl = gt.ap[:, wi * 128:(wi + 1) * 128]
                        s1c, s2c = S13[:, w, c:c + 1], S23[:, w, c:c + 1]
                        emit(DVE, lambda e, gsl=gsl, s1c=s1c: e.tensor_scalar(
                                 out=JK.ap, in0=gsl, scalar1=1.0, scalar2=None, op0=ALU.mult, op1=ALU.add, accum_out=s1c),
                             reads=[gt.res], writes=[JK.res, S1.res])
                        emit(DVE, lambda e, gsl=gsl, s2c=s2c: e.scalar_tensor_tensor(
                                 out=JK.ap, in0=gsl, scalar=1.0, in1=gsl, op0=ALU.mult, op1=ALU.mult, accum_out=s2c),
                             reads=[gt.res], writes=[JK.res, S2.res])
                ws.release(qv)
            for w in range(nW):
                emit(ACT, op_act(JK.ap[:, 0:KD], S13[:, w, :], AF.Copy, accum_out=s1r[:, w:w + 1]),
                     reads=[S1.res], writes=[JK.res, MR.res])
                emit(ACT, op_act(JK.ap[:, 0:KD], S23[:, w, :], AF.Copy, accum_out=s2r[:, w:w + 1]),
                     reads=[S2.res], writes=[JK.res, MR.res])
            emit(DVE, op_ts(mean, s1r, 1.0 / D, None, ALU.mult), reads=[MR.res], writes=[MR.res])
            emit(DVE, op_tt(m2, mean, mean, ALU.mult), reads=[MR.res], writes=[MR.res])
            emit(DVE, op_stt(var, s2r, 1.0 / D, m2, ALU.mult, ALU.subtract), reads=[MR.res], writes=[MR.res])
            emit(ACT, op_act(var, var, AF.Sqrt, bias=EPS), reads=[MR.res], writes=[MR.res])
            emit(DVE, op_recip(rsd, var), reads=[MR.res], writes=[MR.res])
            for c in range(KD):
                qv, wv, rv = ws.next(("cin", i, "v", c))
                for w0 in range(0, nW, 4):
                    wb = wins[w0:w0 + 4]
                    gt = vmm(c, wv, rv, wb)
                    for wi, w in enumerate(wb):
                        emit(DVE, op_ts(VN[:, w, c * 128:(c + 1) * 128], gt.ap[:, wi * 128:(wi + 1) * 128],
                                        mean[:, w:w + 1], rsd[:, w:w + 1], ALU.subtract, ALU.mult),
                             reads=[gt.res, MR.res], writes=[VN_res[w]])
                ws.release(qv)
            tl_in = tiles(hf, lo_in)
            for c in range(KD):
                g = c // cfg.HC
                qu, wu, ru = ws.next(("cin", i, "u", c))
                ub = UB[c % 2]
                b2 = B2[c % 2]
                emit(DVE, op_stt(b2.ap, RS3[:, g, :], pcol(("slnb", i), c), SGB[:, g, :], ALU.mult, ALU.add),
                     reads=[RS.res, ODDP.res, par_res], writes=[b2.res])
                for (sg, c0, n) in tl_in:
                    pu = rotB.next()
                    mm_group(pu.ap[:, :n], [(wu[:, k * 128:(k + 1) * 128], h_sb[:, k, c0:c0 + n]) for k in range(KD)],
                             [ru] + h_res[sg], [pu.res])
                    emit(ACT, op_act(ub.ap[:, c0 - lo_in:c0 - lo_in + n], pu.ap[:, :n], AF.Gelu_apprx_tanh),
                         reads=[pu.res], writes=[ub.res])
                ws.release(qu)
                for w0 in range(0, nW, 4):
                    wb = wins[w0:w0 + 4]
                    pm = rotA.next()
                    rd = [VN_res[w] for w in wb] + [WTb.res]
                    _deps(PE, rd, [pm.res])
                    tok = None
                    for wi, w in enumerate(wb):
                        tok = PE.op(op_mm(pm.ap[:, wi * 128:(wi + 1) * 128], VN[:, w, c * 128:(c + 1) * 128],
                                          WTb3[:, g, :], True, True), signal=(wi == len(wb) - 1))
                    _mark(tok, PE, rd, [pm.res])
                    for wi, w in enumerate(wb):
                        t0 = lo_in + w * 128
                        tw = TWB[w % 2]
                        emit(DVE, op_stt(tw.ap, pm.ap[:, wi * 128:(wi + 1) * 128], pcol(("slng", i), c),
                                         b2.ap, ALU.mult, ALU.add),
                             reads=[pm.res, b2.res, par_res], writes=[tw.res])
                        sgs = [sg for (sg, c0, n) in tl_in if c0 < t0 + 128 and t0 < c0 + n]
                        emit(DVE, op_tt(Gb[:, c, t0 - lo_in:t0 - lo_in + 128], tw.ap, ub.ap[:, t0 - lo_in:t0 - lo_in + 128], ALU.mult),
                             reads=[tw.res, ub.res], writes=[G_res[c][sg] for sg in sgs])
            out_proj(hf, lo_out, lambda kc: ("cout", i, kc), KD,
                     lambda kc, c0, n: Gb[:, kc, c0 - lo_in:c0 - lo_in + n], lambda kc, sg: G_res[kc][sg], 1.0)
            barrier()

        emit(DVE, op_memset(ones32[:], 1.0), writes=[ones_res])
        dma(SP, par[:], d_par, par_sem, par_cnt, 0, writes=[par_res])
        LO_IN = [0, 32, 128, 160]
        LO_OUT = [32, 128, 160, 160]
        for hf in range(2):
            T = TA if hf == 0 else TB
            off = 0 if hf == 0 else TA
            allx = [r for sg in range(NSEG) for r in x_res[sg]]
            dma(SP, x_sb[:, :, 0:T], d_x[:, :, off:off + T], x_sem, x_cnt, 0, writes=allx)
            for l in range(cfg.DEPTH):
                li = LO_IN[l] if hf == 0 else 0
                lo = LO_OUT[l] if hf == 0 else 0
                ffn(hf, l, 1, li)
                if l % 2 == 0:
                    mixer_even(hf, l, li, lo)
                else:
                    mixer_odd(hf, l, li, lo)
                ffn(hf, l, 2, lo)
            lo = HALO if hf == 0 else 0
            rmsnorm(hf, lo, ("fnorm",), inplace=True)
            ooff = 0 if hf == 0 else HALF
            dma(ACT, d_out[:, :, ooff:ooff + HALF], x_sb[:, :, lo:lo + HALF], out_sem, out_cnt, 0, reads=allx)
        assert ws.fetched == 2 * NP, (ws.fetched, NP)
        fin = Tok(out_sem, out_cnt[0], None)
        ACT.wait(fin)

        engs = {"tensor": PE, "scalar": ACT, "vector": DVE, "gpsimd": POOL, "sync": SP}
        with nc.Block() as block:
            for name, E in engs.items():
                def body(e, E=E):
                    for f in E.ops:
                        f(e)
                getattr(block, name)(body)
        stats = {k: len(v.ops) for k, v in engs.items()}
    return nc, stats


_CACHE = {}


def run(cfg, inputs, trace=False):
    inp = {k: np.asarray(v) for k, v in inputs.items()}
    wst = pack_weights(cfg, inp)
    oddp = pack_oddp(cfg, inp)
    in_maps = []
    for c in range(cfg.NCORES):
        in_maps.append({"xT": pack_x(cfg, inp["x"], c), "wst": wst,
                        "par": pack_params(cfg, inp, c), "oddp": oddp})
    nc, stats = build_program(cfg)
    res = run_bass_kernel_spmd(nc, in_maps, core_ids=list(range(cfg.NCORES)), trace=trace)
    out = np.zeros((cfg.BATCH, cfg.SEQ, cfg.D), np.float32)
    for c in range(cfg.NCORES):
        o = res.results[c]["outT"]
        b = c // cfg.CPS
        s0 = (c % cfg.CPS) * cfg.OWN
        out[b, s0:s0 + cfg.OWN, :] = o.transpose(2, 1, 0).reshape(cfg.OWN, cfg.D)
    return out, res, stats


def kernel(**inputs):
    cfg = Cfg()
    out, _, _ = run(cfg, inputs)
    return out
```
